# Optimizing a Trainium2 kernel written in Bass

```python
import jax, jax.numpy as jnp
from jax import lax
import numpy as np

D_MODEL = 1024
BATCH = 4
SEQ = 4096
DEPTH = 4

GRID_W = 64
CTX_LEN = 256
EPS = 1e-6

RNN_WIDTH = D_MODEL
RNN_BLOCKS = 16
RNN_BLOCK = RNN_WIDTH // RNN_BLOCKS
CONV_W = 4
CONV_PAD_L = 2
LRU_C = 8.0

SGU_WIDTH = D_MODEL
SGU_CHUNK = 128
SGU_GROUPS = 8
SGU_GROUP_CH = SGU_WIDTH // SGU_GROUPS

N_HEADS = 16
N_KV_HEADS = 4
Q_PER_KV = N_HEADS // N_KV_HEADS
HEAD_DIM = 64
D_ATTN = N_HEADS * HEAD_DIM
WINDOW = 128
ATT_BLOCK = 128
BAND = ATT_BLOCK + 2 * WINDOW
ATT_SCALE = HEAD_DIM ** -0.5
ROPE_BASE = 10000.0
ROPE_FREQS = HEAD_DIM // 4

N_BRANCH = 3
D_FF = 4 * D_MODEL

OFF_A = 0
OFF_B = OFF_A + RNN_WIDTH
OFF_Q = OFF_B + 2 * SGU_WIDTH
OFF_K = OFF_Q + D_ATTN
OFF_V = OFF_K + N_KV_HEADS * HEAD_DIM
OFF_G = OFF_V + N_KV_HEADS * HEAD_DIM
IN_WIDTH = OFF_G + N_BRANCH * D_MODEL

kernel_name = 'hybrid_prefix_diffusion_block'


def rms_norm(t, g):
    tf = t.astype(jnp.float32)
    y = tf * lax.rsqrt(jnp.mean(tf * tf, axis=-1, keepdims=True) + EPS)
    return (y * g.astype(jnp.float32)).astype(t.dtype)


def layer_norm(t, g, b):
    tf = t.astype(jnp.float32)
    mu = jnp.mean(tf, axis=-1, keepdims=True)
    var = jnp.mean(jnp.square(tf - mu), axis=-1, keepdims=True)
    y = (tf - mu) * lax.rsqrt(var + EPS) * g.astype(jnp.float32) + b.astype(jnp.float32)
    return y.astype(t.dtype)


def modulate(h, shift, scale):
    return h * (1 + scale) + shift


def short_conv(t, w, b):
    L = t.shape[1]
    tp = jnp.pad(t, ((0, 0), (CONV_PAD_L, CONV_W - 1 - CONV_PAD_L), (0, 0)))
    out = b
    for k in range(CONV_W):
        out = out + w[k] * tp[:, k:k + L]
    return out


def lru_coeffs(t, wa, ba, wx, bx, lam):
    nb, L, C = t.shape
    tb = t.reshape(nb, L, RNN_BLOCKS, RNN_BLOCK)
    r = jax.nn.sigmoid(jnp.einsum('blhi,hij->blhj', tb, wa).reshape(nb, L, C) + ba)
    i = jax.nn.sigmoid(jnp.einsum('blhi,hij->blhj', tb, wx).reshape(nb, L, C) + bx)
    log_a = -LRU_C * r * jax.nn.softplus(-lam)
    a = jnp.exp(log_a)
    u = t * i * jnp.sqrt(-jnp.expm1(2 * log_a))
    return a, u


def linear_scan(a, u, h0):
    u = u.at[:, 0].add(a[:, 0] * h0)

    def combine(left, right):
        al, ul = left
        ar, ur = right
        return al * ar, ar * ul + ur

    _, h = lax.associative_scan(combine, (a, u), axis=1)
    return h


def rglru_bidir(xa, h0_f, h0_b, conv_w, conv_b, wa, ba, wx, bx, lam):
    xc = short_conv(xa, conv_w, conv_b)
    a_f, u_f = lru_coeffs(xc, wa[0], ba[0], wx[0], bx[0], lam[0])
    a_b, u_b = lru_coeffs(xc, wa[1], ba[1], wx[1], bx[1], lam[1])
    h_f = linear_scan(a_f, u_f, h0_f)
    h_b = jnp.flip(linear_scan(jnp.flip(a_b, 1), jnp.flip(u_b, 1), h0_b), 1)
    return h_f + h_b, h_f[:, -1], h_b[:, 0]


def spatial_gating(z, ln_g, ln_b, w_s, b_s):
    nb, L, _ = z.shape
    u, v = jnp.split(z, 2, axis=-1)
    v = layer_norm(v, ln_g, ln_b)
    nc = L // SGU_CHUNK
    vb = v.reshape(nb, nc, SGU_CHUNK, SGU_GROUPS, SGU_GROUP_CH)
    mixed = jnp.einsum('gpq,bnqgc->bnpgc', w_s, vb) + b_s.T[None, None, :, :, None]
    return u * mixed.reshape(nb, L, SGU_WIDTH)


def axial_rope_tables(rows, dtype):
    row = jnp.repeat(jnp.arange(rows), GRID_W).astype(jnp.float32)
    col = jnp.tile(jnp.arange(GRID_W), rows).astype(jnp.float32)
    inv = jnp.power(ROPE_BASE, -jnp.arange(ROPE_FREQS, dtype=jnp.float32) / ROPE_FREQS)
    ang = jnp.concatenate([row[:, None] * inv, col[:, None] * inv], axis=-1)
    return jnp.cos(ang).astype(dtype), jnp.sin(ang).astype(dtype)


def apply_axial_rope(t, cos, sin):
    shp = t.shape
    tr = t.reshape(*shp[:-1], 2, 2, ROPE_FREQS)
    t1, t2 = tr[..., 0, :], tr[..., 1, :]
    cs = cos.reshape(shp[1], 2, ROPE_FREQS)[None, :, None]
    sn = sin.reshape(shp[1], 2, ROPE_FREQS)[None, :, None]
    return jnp.stack([t1 * cs - t2 * sn, t2 * cs + t1 * sn], axis=-2).reshape(shp)


def sink_attend(q, ks, vs, sink, mask=None):
    logits = [jnp.einsum('bqkgd,bjkd->bkgqj', q, k).astype(jnp.float32) for k in ks]
    if mask is not None:
        logits[0] = jnp.where(mask, logits[0], -jnp.inf)
    sink_col = jnp.broadcast_to(sink.reshape(N_KV_HEADS, Q_PER_KV, 1, 1).astype(jnp.float32),
                                logits[0].shape[:-1] + (1,))
    p = jax.nn.softmax(jnp.concatenate(logits + [sink_col], axis=-1), axis=-1)
    out = None
    start = 0
    for k, v in zip(ks, vs):
        n_k = k.shape[1]
        part = jnp.einsum('bkgqj,bjkd->bqkgd', p[..., start:start + n_k].astype(v.dtype), v)
        out = part if out is None else out + part
        start += n_k
    return out


def windowed_attention(q, k, v, k_ctx, v_ctx, sink):
    nb, S, _, _ = q.shape
    n_blk = S // ATT_BLOCK
    qb = jnp.moveaxis(q.reshape(nb, n_blk, ATT_BLOCK, N_KV_HEADS, Q_PER_KV, HEAD_DIM), 1, 0)
    pad = ((0, 0), (WINDOW, WINDOW), (0, 0), (0, 0))
    kp = jnp.pad(k, pad)
    vp = jnp.pad(v, pad)
    i = jnp.arange(ATT_BLOCK)[:, None]
    j = jnp.arange(BAND)[None, :]
    in_window = jnp.abs(j - WINDOW - i) <= WINDOW

    def block(args):
        n, q_n = args
        k_n = lax.dynamic_slice_in_dim(kp, n * ATT_BLOCK, BAND, axis=1)
        v_n = lax.dynamic_slice_in_dim(vp, n * ATT_BLOCK, BAND, axis=1)
        tok = n * ATT_BLOCK - WINDOW + j
        mask = in_window & (tok >= 0) & (tok < S)
        return sink_attend(q_n, [k_n, k_ctx], [v_n, v_ctx], sink, mask)

    out = lax.map(block, (jnp.arange(n_blk), qb))
    return jnp.moveaxis(out, 0, 1).reshape(nb, S, D_ATTN)


def merge_branches(gate_logits, ya, yb, yc, w_branch, w_out):
    ga, gb, gc = jnp.split(jax.nn.sigmoid(gate_logits), N_BRANCH, axis=-1)
    m = ga * (ya @ w_branch[0]) + gb * (yb @ w_branch[1]) + gc * (yc @ w_branch[2])
    return m @ w_out


def sq_relu_mlp(h, w1, w2):
    return jnp.square(jax.nn.relu(h @ w1)) @ w2


def setup_inputs(seed: int = 0) -> dict:
    key = jax.random.key(seed)
    ks = jax.random.split(key, 27)

    def nrm(k, shape, scale):
        return jax.random.normal(k, shape, jnp.float32) * scale

    u = jax.random.uniform(ks[14], (DEPTH, 2, RNN_WIDTH), jnp.float32, 0.9, 0.999)
    s = u ** (1.0 / LRU_C)
    lru_lambda = jnp.log(s) - jnp.log1p(-s)
    return {
        'x': nrm(ks[0], (BATCH, SEQ, D_MODEL), 1.0),
        'c': nrm(ks[1], (BATCH, D_MODEL), 1.0),
        'ctx': nrm(ks[2], (BATCH, CTX_LEN, D_MODEL), 1.0),
        'c_ctx': nrm(ks[3], (D_MODEL,), 1.0),
        'w_mod': nrm(ks[4], (DEPTH, D_MODEL, 6 * D_MODEL), 0.5 * D_MODEL ** -0.5),
        'b_mod': nrm(ks[5], (DEPTH, 6 * D_MODEL), 0.01),
        'g_norm1': 1.0 + nrm(ks[6], (DEPTH, D_MODEL), 0.01),
        'w_in': nrm(ks[7], (DEPTH, D_MODEL, IN_WIDTH), D_MODEL ** -0.5),
        'conv_w': nrm(ks[8], (DEPTH, CONV_W, RNN_WIDTH), CONV_W ** -0.5),
        'conv_b': nrm(ks[9], (DEPTH, RNN_WIDTH), 0.01),
        'lru_wa': nrm(ks[10], (DEPTH, 2, RNN_BLOCKS, RNN_BLOCK, RNN_BLOCK), RNN_BLOCK ** -0.5),
        'lru_ba': nrm(ks[11], (DEPTH, 2, RNN_WIDTH), 0.01),
        'lru_wx': nrm(ks[12], (DEPTH, 2, RNN_BLOCKS, RNN_BLOCK, RNN_BLOCK), RNN_BLOCK ** -0.5),
        'lru_bx': nrm(ks[13], (DEPTH, 2, RNN_WIDTH), 0.01),
        'lru_lambda': lru_lambda,
        'sgu_ln_g': 1.0 + nrm(ks[15], (DEPTH, SGU_WIDTH), 0.01),
        'sgu_ln_b': nrm(ks[16], (DEPTH, SGU_WIDTH), 0.01),
        'sgu_w': nrm(ks[17], (DEPTH, SGU_GROUPS, SGU_CHUNK, SGU_CHUNK), SGU_CHUNK ** -0.5),
        'sgu_b': 1.0 + nrm(ks[18], (DEPTH, SGU_GROUPS, SGU_CHUNK), 0.01),
        'q_norm_g': 1.0 + nrm(ks[19], (DEPTH, HEAD_DIM), 0.01),
        'k_norm_g': 1.0 + nrm(ks[20], (DEPTH, HEAD_DIM), 0.01),
        'sink': nrm(ks[21], (DEPTH, N_HEADS), 1.0),
        'w_branch': nrm(ks[22], (DEPTH, N_BRANCH, D_MODEL, D_MODEL), D_MODEL ** -0.5),
        'w_out': nrm(ks[23], (DEPTH, D_MODEL, D_MODEL), D_MODEL ** -0.5),
        'g_norm2': 1.0 + nrm(ks[24], (DEPTH, D_MODEL), 0.01),
        'w_ff1': nrm(ks[25], (DEPTH, D_MODEL, D_FF), D_MODEL ** -0.5),
        'w_ff2': nrm(ks[26], (DEPTH, D_FF, D_MODEL), D_FF ** -0.5),
    }


def reference(x, c, ctx, c_ctx, w_mod, b_mod, g_norm1, w_in, conv_w, conv_b, lru_wa, lru_ba,
              lru_wx, lru_bx, lru_lambda, sgu_ln_g, sgu_ln_b, sgu_w, sgu_b, q_norm_g, k_norm_g,
              sink, w_branch, w_out, g_norm2, w_ff1, w_ff2):
    n_batch, n_tok, _ = x.shape
    n_ctx = ctx.shape[1]
    ROWS = n_tok // GRID_W
    cos, sin = axial_rope_tables(ROWS, x.dtype)
    cond_x = jax.nn.silu(c)
    cond_c = jax.nn.silu(c_ctx)
    cx = ctx
    for l in range(DEPTH):
        last = l == DEPTH - 1
        mod_x = (cond_x @ w_mod[l] + b_mod[l])[:, None, :]
        mod_c = cond_c @ w_mod[l] + b_mod[l]
        sh1x, sc1x, ga1x, sh2x, sc2x, ga2x = jnp.split(mod_x, 6, axis=-1)
        sh1c, sc1c, ga1c, sh2c, sc2c, ga2c = jnp.split(mod_c, 6, axis=-1)
        lru_p = (conv_w[l], conv_b[l], lru_wa[l], lru_ba[l], lru_wx[l], lru_bx[l], lru_lambda[l])
        sgu_p = (sgu_ln_g[l], sgu_ln_b[l], sgu_w[l], sgu_b[l])

        zc = modulate(rms_norm(cx, g_norm1[l]), sh1c, sc1c) @ w_in[l]
        kc = rms_norm(zc[..., OFF_K:OFF_V].reshape(n_batch, n_ctx, N_KV_HEADS, HEAD_DIM), k_norm_g[l])
        vc = zc[..., OFF_V:OFF_G].reshape(n_batch, n_ctx, N_KV_HEADS, HEAD_DIM)
        h0 = jnp.zeros((n_batch, RNN_WIDTH), zc.dtype)
        ya_c, hf_c, hb_c = rglru_bidir(zc[..., OFF_A:OFF_B], h0, h0, *lru_p)

        zx = modulate(rms_norm(x, g_norm1[l]), sh1x, sc1x) @ w_in[l]
        ya_x, _, _ = rglru_bidir(zx[..., OFF_A:OFF_B], hf_c, hb_c, *lru_p)
        yb_x = spatial_gating(jax.nn.gelu(zx[..., OFF_B:OFF_Q]), *sgu_p)
        q = rms_norm(zx[..., OFF_Q:OFF_K].reshape(n_batch, n_tok, N_HEADS, HEAD_DIM), q_norm_g[l])
        k = rms_norm(zx[..., OFF_K:OFF_V].reshape(n_batch, n_tok, N_KV_HEADS, HEAD_DIM), k_norm_g[l])
        v = zx[..., OFF_V:OFF_G].reshape(n_batch, n_tok, N_KV_HEADS, HEAD_DIM)
        q = apply_axial_rope(q, cos, sin) * ATT_SCALE
        k = apply_axial_rope(k, cos, sin)
        yc_x = windowed_attention(q, k, v, kc, vc, sink[l])
        x = x + ga1x * merge_branches(zx[..., OFF_G:], ya_x, yb_x, yc_x, w_branch[l], w_out[l])
        x = x + ga2x * sq_relu_mlp(modulate(rms_norm(x, g_norm2[l]), sh2x, sc2x), w_ff1[l], w_ff2[l])

        if not last:
            yb_c = spatial_gating(jax.nn.gelu(zc[..., OFF_B:OFF_Q]), *sgu_p)
            qc = rms_norm(zc[..., OFF_Q:OFF_K].reshape(n_batch, n_ctx, N_KV_HEADS, Q_PER_KV, HEAD_DIM),
                          q_norm_g[l]) * ATT_SCALE
            yc_c = sink_attend(qc, [kc], [vc], sink[l]).reshape(n_batch, n_ctx, D_ATTN)
            cx = cx + ga1c * merge_branches(zc[..., OFF_G:], ya_c, yb_c, yc_c, w_branch[l], w_out[l])
            cx = cx + ga2c * sq_relu_mlp(modulate(rms_norm(cx, g_norm2[l]), sh2c, sc2c), w_ff1[l], w_ff2[l])
    return x
```

```python
import contextlib
import numpy as np
import ml_dtypes
import concourse.bass as bass
import concourse.mybir as mybir
from concourse.bass_utils import run_bass_kernel_spmd

F32 = mybir.dt.float32
BF16 = mybir.dt.bfloat16
AF = mybir.ActivationFunctionType
ALU = mybir.AluOpType

D = 1024
S = 4096
CTX = 256
TT = S + CTX
NL = 4
INW = 7680
DFF = 4096
OFF_U = 1024
OFF_SV = 2048
OFF_Q = 3072
OFF_K = 4096
OFF_V = 4352
OFF_G = 4608
NSP = 162
EPS = 1e-6
GRID_W = 64
TILES = [(0, 256, 1)] + [(256 + 512 * i, 512, 0) for i in range(8)]


class T:
    __slots__ = ("ap", "w", "r", "name")

    def __init__(self, ap=None, name=""):
        self.ap = ap
        self.w = {}
        self.r = {}
        self.name = name


class FW:
    def __init__(self, nc, n_dma_sems=12):
        self.nc = nc
        self.stack = [contextlib.ExitStack()]
        self.engs = {"pe": nc.tensor, "act": nc.scalar, "dve": nc.vector,
                     "pool": nc.gpsimd, "sp": nc.sync}
        self.sems = {}
        self.cnt = {}
        self.waited = {}
        for e in self.engs:
            self.sems[e] = self.stack[0].enter_context(nc.semaphore("s_" + e))
            self.cnt[e] = 0
        self.dma_pool = {}
        self.dma_rr = {}
        for q in ("sp", "pool"):
            ks = []
            for i in range(n_dma_sems):
                k = "d_%s_%d" % (q, i)
                self.sems[k] = self.stack[0].enter_context(nc.semaphore(k))
                self.cnt[k] = 0
                ks.append(k)
            self.dma_pool[q] = ks
            self.dma_rr[q] = 0
        self.uid = 0

    def sbuf(self, name, shape, dtype):
        self.uid += 1
        t = self.stack[-1].enter_context(self.nc.sbuf_tensor("%s_%d" % (name, self.uid), list(shape), dtype))
        return T(t, name)

    def psum(self, name, shape, dtype=F32):
        t = self.stack[-1].enter_context(self.nc.psum_tensor(name, list(shape), dtype))
        return T(t, name)

    def _wait(self, e, k, v):
        if self.waited.get((e, k), 0) >= v:
            return
        self.engs[e].wait_ge(self.sems[k], v)
        self.waited[(e, k)] = v

    def _deps(self, e, reads, writes):
        for t in reads:
            for k, v in t.w.items():
                self._wait(e, k, v)
        for t in writes:
            for k, v in t.w.items():
                self._wait(e, k, v)
            for k, v in t.r.items():
                self._wait(e, k, v)

    def _mark(self, k, v, reads, writes):
        for t in reads:
            if t.r.get(k, 0) < v:
                t.r[k] = v
        for t in writes:
            if t.w.get(k, 0) < v:
                t.w[k] = v

    def op(self, e, fn, reads=(), writes=()):
        self._deps(e, reads, writes)
        inst = fn(self.engs[e])
        self.cnt[e] += 1
        inst.then_inc(self.sems[e], 1)
        self._mark(e, self.cnt[e], reads, writes)
        return inst

    def dma(self, q, out, in_, reads=(), writes=(), **kw):
        self._deps(q, reads, writes)
        ks = self.dma_pool[q]
        k = ks[self.dma_rr[q] % len(ks)]
        self.dma_rr[q] += 1
        if self.cnt[k] > 0:
            self._wait(q, k, self.cnt[k])
        inst = self.engs[q].dma_start(out=out, in_=in_, **kw)
        self.cnt[k] += 16
        inst.then_inc(self.sems[k], 16)
        self._mark(k, self.cnt[k], reads, writes)
        return inst

    def barrier(self, engines=None):
        for e in (engines or self.engs):
            for k, v in self.cnt.items():
                if v > 0:
                    self._wait(e, k, v)

    @contextlib.contextmanager
    def scope(self):
        self.stack.append(contextlib.ExitStack())
        try:
            yield
        finally:
            self.barrier()
            self.stack.pop().close()

    def close(self):
        while self.stack:
            self.stack.pop().close()


def build(n_layers=NL, dbg=False):
    nc = bass.Bass("TRN2", target_bir_lowering=False)

    def din(name, shape, dt=F32):
        return nc.dram_tensor(name, list(shape), dt, kind="ExternalInput").ap()

    def dscr(name, shape, dt, out=False):
        kind = "ExternalOutput" if (out and dbg) else "Internal"
        return nc.dram_tensor(name, list(shape), dt, kind=kind).ap()

    xT_in = din("xT", [D, TT])
    cond_in = din("cond", [128, 8, 2])
    sp_in = din("sp_all", [128, NL, NSP])
    w_mod = din("w_mod", [NL, D, 6 * D])
    w_in = din("w_in", [NL, D, INW])
    lru_wa = din("lru_wa", [NL, 2, 16, 64, 64])
    lru_wx = din("lru_wx", [NL, 2, 16, 64, 64])
    sgu_g = din("sgu_ln_g", [NL, D])
    sgu_bb = din("sgu_ln_b", [NL, D])
    sgu_bs = din("sgu_b", [NL, D])
    sgu_wT = din("sgu_wT", [NL, 8, 128, 128])
    w_branch = din("w_branch", [NL, 3, D, D])
    w_out = din("w_out", [NL, D, D])
    w_ff1 = din("w_ff1", [NL, D, DFF])
    w_ff2 = din("w_ff2", [NL, DFF, D])
    cbf_in = din("cbf", [128, 4, 128], BF16)
    masks_in = din("masks", [128, 2, 256], BF16)
    cos_in = din("cosT", [128, TT])
    sin_in = din("sinT", [128, TT])
    outT = nc.dram_tensor("outT", [D, S], F32, kind="ExternalOutput").ap()

    xres = dscr("xres", [D, TT], F32, out=True)
    xn_d = dscr("xn_d", [D, TT], BF16, out=True)
    xa_d = dscr("xa_d", [D, TT], BF16, out=True)
    ya_d = dscr("ya_d", [D, TT], BF16, out=True)
    kT_d = dscr("kT_d", [4, 128, TT], BF16, out=True)
    v_d = dscr("v_d", [TT, 256], BF16, out=True)
    dbg_yb = dscr("dbg_yb", [D, TT], BF16, out=True) if dbg else None
    dbg_yc = dscr("dbg_yc", [D, TT], BF16, out=True) if dbg else None
    dbg_x1 = dscr("dbg_x1", [D, TT], F32, out=True) if dbg else None
    win_bf = [dscr("win_bf%d" % l, [D, INW], BF16) for l in range(n_layers)]
    wk2_bf = [dscr("wk2_bf%d" % l, [D, 512], BF16) for l in range(n_layers)]
    wbr_bf = [dscr("wbr_bf%d" % l, [3, D, D], BF16) for l in range(n_layers)]
    wout_bf = [dscr("wout_bf%d" % l, [D, D], BF16) for l in range(n_layers)]
    wff1_bf = [dscr("wff1_bf%d" % l, [D, DFF], BF16) for l in range(n_layers)]
    wff2_bf = [dscr("wff2_bf%d" % l, [DFF, D], BF16) for l in range(n_layers)]

    fw = FW(nc)
    NT = len(TILES)
    T_win = [T(name="win") for _ in range(n_layers)]
    T_wk2 = [T(name="wk2") for _ in range(n_layers)]
    T_wbr = [T(name="wbr") for _ in range(n_layers)]
    T_wout = [T(name="wout") for _ in range(n_layers)]
    T_wff1 = [T(name="wff1") for _ in range(n_layers)]
    T_wff2 = [T(name="wff2") for _ in range(n_layers)]
    T_xres = [T(name="xres%d" % i) for i in range(NT)]
    T_xn = [T(name="xn%d" % i) for i in range(NT)]
    T_xa = [T(name="xa%d" % i) for i in range(NT)]
    T_ya = [T(name="ya%d" % i) for i in range(8)]
    T_kv = [T(name="kv%d" % i) for i in range(NT)]
    T_out = T(name="out")
    T_dbg = T(name="dbg")

    cbf = fw.sbuf("cbf", [128, 4, 128], BF16)
    ones_bf = cbf.ap[:, 0, :]
    bones_bf = cbf.ap[:, 1, :]
    rmat_bf = cbf.ap[:, 2, :]
    ident_bf = cbf.ap[:, 3, :]
    masks = fw.sbuf("masks", [128, 2, 256], BF16)
    spl = fw.sbuf("spl", [128, NL, NSP], F32)
    condf = fw.sbuf("condf", [128, 8, 2], F32)
    condb = fw.sbuf("condb", [128, 8, 2], BF16)
    gw = fw.sbuf("gw", [128, 32, 128], BF16)
    cd = fw.sbuf("cd", [128, 32, 128], BF16)
    modT = fw.sbuf("modT", [128, 48, 2], F32)
    gs1 = fw.sbuf("gs1", [128, 8, 2], F32)
    gs2 = fw.sbuf("gs2", [128, 8, 2], F32)
    lam_t = fw.sbuf("lam_t", [128, 16], F32)
    clam = fw.sbuf("clam", [128, 16], F32)
    hclam = fw.sbuf("hclam", [128, 16], F32)
    hba = fw.sbuf("hba", [128, 16], F32)
    hbx = fw.sbuf("hbx", [128, 16], F32)
    esink = fw.sbuf("esink", [128, 8], F32)
    gqs = fw.sbuf("gqs", [128, 1], F32)
    wsT = fw.sbuf("wsT", [128, 8, 128], BF16)
    lng_bc = fw.sbuf("lng_bc", [128, D], F32)
    lnb_bc = fw.sbuf("lnb_bc", [128, D], F32)
    bs_bc = fw.sbuf("bs_bc", [128, 8, 128], F32)
    banks = [fw.psum("bank%d" % i, [128, 512]) for i in range(8)]
    bank_rr = [0]

    def bank():
        b = banks[bank_rr[0] % 8]
        bank_rr[0] += 1
        return b

    fw.dma("sp", cbf.ap[:], cbf_in, writes=[cbf])
    fw.dma("sp", masks.ap[:], masks_in, writes=[masks])
    fw.dma("sp", spl.ap[:], sp_in, writes=[spl])
    fw.dma("sp", condf.ap[:], cond_in, writes=[condf])
    fw.op("act", lambda e: e.activation(condb.ap[:], condf.ap[:], AF.Silu), reads=[condf], writes=[condb])
    fw.op("pool", lambda e: e.memset(gw.ap[:], 0.0), writes=[gw])

    ev_rr = [0]

    def evac_copy(out_ap, in_ap, reads, writes):
        ev_rr[0] += 1
        if ev_rr[0] % 2:
            fw.op("act", lambda e: e.activation(out_ap, in_ap, AF.Copy), reads=reads, writes=writes)
        else:
            fw.op("dve", lambda e: e.tensor_copy(out_ap, in_ap), reads=reads, writes=writes)

    def precast(l):
        for r in range(8):
            fw.dma("pool", win_bf[l][r * 128:(r + 1) * 128, :], w_in[l, r * 128:(r + 1) * 128, :], writes=[T_win[l]])
        for h in range(4):
            for half in range(2):
                fw.dma("pool", wk2_bf[l][:, h * 128 + half * 64:h * 128 + half * 64 + 64],
                       w_in[l, :, OFF_K + 64 * h:OFF_K + 64 * h + 64], writes=[T_wk2[l]])
        for br in range(3):
            for r in range(2):
                fw.dma("pool", wbr_bf[l][br, r * 512:(r + 1) * 512, :], w_branch[l, br, r * 512:(r + 1) * 512, :],
                       writes=[T_wbr[l]])
        for r in range(2):
            fw.dma("pool", wout_bf[l][r * 512:(r + 1) * 512, :], w_out[l, r * 512:(r + 1) * 512, :], writes=[T_wout[l]])
        for r in range(8):
            fw.dma("pool", wff1_bf[l][r * 128:(r + 1) * 128, :], w_ff1[l, r * 128:(r + 1) * 128, :], writes=[T_wff1[l]])
        for r in range(8):
            fw.dma("pool", wff2_bf[l][r * 512:(r + 1) * 512, :], w_ff2[l, r * 512:(r + 1) * 512, :], writes=[T_wff2[l]])

    precast(0)

    class Pools:
        pass

    P = Pools()

    def make_pools(nw=4, n32=12, n16=6):
        P.w = [fw.sbuf("wp", [128, 8, 512], BF16) for _ in range(nw)]
        P.wi = 0
        P.s32 = [fw.sbuf("s32", [128, 512], F32) for _ in range(n32)]
        P.i32 = 0
        P.s16 = [fw.sbuf("s16", [128, 512], BF16) for _ in range(n16)]
        P.i16 = 0

    def getw():
        t = P.w[P.wi % len(P.w)]
        P.wi += 1
        return t

    def g32():
        t = P.s32[P.i32 % len(P.s32)]
        P.i32 += 1
        return t

    def g16():
        t = P.s16[P.i16 % len(P.s16)]
        P.i16 += 1
        return t

    def load_w(src2d, srcT, ncols=512):
        wt = getw()
        fw.dma("sp", wt.ap[:, :, 0:ncols], src2d.rearrange("(k p) n -> p k n", p=128), reads=[srcT], writes=[wt])
        return wt

    def proj(bk, n, wt, col0, act, nk=8, m=128):
        def f(e):
            r = None
            for k in range(nk):
                r = e.matmul(bk.ap[0:m, 0:n], wt.ap[:, k, col0:col0 + m], act.ap[:, k, 0:n],
                             start=(k == 0), stop=(k == nk - 1))
            return r
        fw.op("pe", f, reads=[wt, act], writes=[bk])

    def rstd_from_bank(bk, n, scale):
        ln = g32()
        fw.op("act", lambda e: e.activation(ln.ap[:, 0:n], bk.ap[:, 0:n], AF.Ln, bias=eps_t.ap[:, 0:1], scale=scale),
              reads=[bk, eps_t], writes=[ln])
        rs = g32()
        fw.op("act", lambda e: e.activation(rs.ap[:, 0:n], ln.ap[:, 0:n], AF.Exp, scale=-0.5), reads=[ln], writes=[rs])
        return rs

    eps_t = fw.sbuf("eps_t", [128, 1], F32)
    fw.op("dve", lambda e: e.memset(eps_t.ap[:], EPS), writes=[eps_t])

    def norm_mod(xt, n, gs_t, sh_col0, s, out_t):
        sq = g16s8()
        fw.op("act", lambda e: e.activation(sq.ap[:, :, 0:n], xt.ap[:, :, 0:n], AF.Square), reads=[xt], writes=[sq])
        bk = bank()

        def f(e):
            r = None
            for c in range(8):
                r = e.matmul(bk.ap[:, 0:n], ones_bf, sq.ap[:, c, 0:n], start=(c == 0), stop=(c == 7))
            return r
        fw.op("pe", f, reads=[sq, cbf], writes=[bk])
        rs = rstd_from_bank(bk, n, 1.0 / D)
        for c in range(8):
            tmp = g32()
            fw.op("dve", lambda e: e.scalar_tensor_tensor(tmp.ap[:, 0:n], xt.ap[:, c, 0:n], gs_t.ap[:, c, s:s + 1],
                                                            rs.ap[:, 0:n], ALU.mult, ALU.mult),
                  reads=[xt, gs_t, rs], writes=[tmp])
            fw.op("act", lambda e: e.activation(out_t.ap[:, c, 0:n], tmp.ap[:, 0:n], AF.Identity,
                                                bias=modT.ap[:, sh_col0 + c, s:s + 1]),
                  reads=[tmp, modT], writes=[out_t])

    def g16s8():
        return P.sq8

    def hnr_group(items, n, act, cos_t, sin_t):
        st = []
        for (wt, col0, g_ap, g_T, out_ap, out_T) in items:
            bk = bank()
            proj(bk, n, wt, col0, act)
            raw = g32()
            fw.op("act", lambda e: e.activation(raw.ap[:, 0:n], bk.ap[:, 0:n], AF.Copy), reads=[bk], writes=[raw])
            sq = g16()
            fw.op("act", lambda e: e.activation(sq.ap[:, 0:n], bk.ap[:, 0:n], AF.Square), reads=[bk], writes=[sq])
            b2 = bank()
            fw.op("pe", lambda e: e.matmul(b2.ap[:, 0:n], bones_bf, sq.ap[:, 0:n], start=True, stop=True),
                  reads=[sq, cbf], writes=[b2])
            st.append({"raw": raw, "b2": b2})
        for d_, (wt, col0, g_ap, g_T, out_ap, out_T) in zip(st, items):
            rq = rstd_from_bank(d_["b2"], n, 1.0 / 64)
            raw = d_["raw"]
            qn = g32()
            fw.op("dve", lambda e: e.scalar_tensor_tensor(qn.ap[:, 0:n], raw.ap[:, 0:n], g_ap, rq.ap[:, 0:n],
                                                            ALU.mult, ALU.mult), reads=[raw, rq, g_T], writes=[qn])
            qnb = g16()
            fw.op("pool", lambda e: e.tensor_copy(qnb.ap[:, 0:n], qn.ap[:, 0:n]), reads=[qn], writes=[qnb])
            b3 = bank()
            fw.op("pe", lambda e: e.matmul(b3.ap[:, 0:n], rmat_bf, qnb.ap[:, 0:n], start=True, stop=True),
                  reads=[qnb, cbf], writes=[b3])
            d_["qn"] = qn
            d_["b3"] = b3
        for d_, (wt, col0, g_ap, g_T, out_ap, out_T) in zip(st, items):
            qn = d_["qn"]
            b3 = d_["b3"]
            t1 = g32()
            fw.op("pool", lambda e: e.tensor_tensor(t1.ap[:, 0:n], qn.ap[:, 0:n], cos_t.ap[:, 0:n], ALU.mult),
                  reads=[qn, cos_t], writes=[t1])
            t2 = g32()
            fw.op("dve", lambda e: e.tensor_tensor(t2.ap[:, 0:n], b3.ap[:, 0:n], sin_t.ap[:, 0:n], ALU.mult),
                  reads=[b3, sin_t], writes=[t2])
            if isinstance(out_ap, tuple):
                fw.op("dve", lambda e: e.tensor_tensor(out_ap[0], t1.ap[0:64, 0:n], t2.ap[0:64, 0:n], ALU.add),
                      reads=[t1, t2], writes=[out_T])
                fw.op("dve", lambda e: e.tensor_tensor(out_ap[1], t1.ap[64:128, 0:n], t2.ap[64:128, 0:n], ALU.add),
                      reads=[t1, t2], writes=[out_T])
            else:
                fw.op("dve", lambda e: e.tensor_tensor(out_ap, t1.ap[:, 0:n], t2.ap[:, 0:n], ALU.add),
                      reads=[t1, t2], writes=[out_T])

    for l in range(n_layers):
        last = (l == n_layers - 1)
        x_src = xT_in if l == 0 else xres

        with fw.scope():
            for typ, src in ((0, lru_wa), (1, lru_wx)):
                for d in range(2):
                    for half in range(2):
                        gi = (d * 2 + typ) * 8
                        dst = gw.ap[half * 64:(half + 1) * 64, gi:gi + 8, half * 64:(half + 1) * 64]
                        srcap = src[l, d].rearrange("(c two) i j -> two i c j", two=2)[half]
                        fw.dma("pool", dst, srcap, writes=[gw])
            fw.dma("pool", wsT.ap[:], sgu_wT[l].rearrange("g q p -> q g p"), writes=[wsT])
            fw.dma("sp", lng_bc.ap[:], sgu_g[l:l + 1, :].to_broadcast([128, D]), writes=[lng_bc])
            fw.dma("sp", lnb_bc.ap[:], sgu_bb[l:l + 1, :].to_broadcast([128, D]), writes=[lnb_bc])
            fw.dma("sp", bs_bc.ap[:].rearrange("p g q -> p (g q)"), sgu_bs[l:l + 1, :].to_broadcast([128, D]),
                   writes=[bs_bc])
            wms = [fw.sbuf("wm", [128, 8, 1024], BF16) for _ in range(2)]
            bkm = bank()
            for sec in range(6):
                wm = wms[sec % 2]
                fw.dma("pool", wm.ap[:], w_mod[l, :, sec * 1024:(sec + 1) * 1024].rearrange("(k p) n -> p k n", p=128),
                       writes=[wm])

                def fm(e, wm=wm, sec=sec):
                    r = None
                    for jj in range(8):
                        j = sec * 8 + jj
                        for k in range(8):
                            r = e.matmul(bkm.ap[:, 2 * j:2 * j + 2], wm.ap[:, k, jj * 128:(jj + 1) * 128],
                                         condb.ap[:, k, :], start=(k == 0), stop=(k == 7))
                    return r
                fw.op("pe", fm, reads=[wm, condb], writes=[bkm])
            fw.op("dve", lambda e: e.tensor_tensor(modT.ap[:], bkm.ap[:, 0:96].rearrange("p (j s) -> p j s", s=2),
                                                   spl.ap[:, l, 0:48].unsqueeze(2).to_broadcast([128, 48, 2]), ALU.add),
                  reads=[bkm, spl], writes=[modT])
            fw.op("dve", lambda e: e.scalar_tensor_tensor(gs1.ap[:], modT.ap[:, 8:16, :], 1.0,
                                                            spl.ap[:, l, 48:56].unsqueeze(2).to_broadcast([128, 8, 2]),
                                                            ALU.add, ALU.mult), reads=[modT, spl], writes=[gs1])
            fw.op("dve", lambda e: e.scalar_tensor_tensor(gs2.ap[:], modT.ap[:, 32:40, :], 1.0,
                                                            spl.ap[:, l, 56:64].unsqueeze(2).to_broadcast([128, 8, 2]),
                                                            ALU.add, ALU.mult), reads=[modT, spl], writes=[gs2])
            fw.op("act", lambda e: e.activation(lam_t.ap[:], spl.ap[:, l, 136:152], AF.Sigmoid), reads=[spl], writes=[lam_t])
            fw.op("act", lambda e: e.activation(lam_t.ap[:], lam_t.ap[:], AF.Ln), reads=[lam_t], writes=[lam_t])
            fw.op("dve", lambda e: e.tensor_scalar(clam.ap[:], lam_t.ap[:], 8.0, None, ALU.mult), reads=[lam_t], writes=[clam])
            fw.op("dve", lambda e: e.tensor_scalar(hclam.ap[:], lam_t.ap[:], 4.0, None, ALU.mult), reads=[lam_t], writes=[hclam])
            fw.op("dve", lambda e: e.tensor_scalar(hba.ap[:], spl.ap[:, l, 104:120], 0.5, None, ALU.mult), reads=[spl], writes=[hba])
            fw.op("dve", lambda e: e.tensor_scalar(hbx.ap[:], spl.ap[:, l, 120:136], 0.5, None, ALU.mult), reads=[spl], writes=[hbx])
            fw.op("act", lambda e: e.activation(esink.ap[:], spl.ap[:, l, 154:162], AF.Exp), reads=[spl], writes=[esink])
            fw.op("dve", lambda e: e.tensor_scalar(gqs.ap[:], spl.ap[:, l, 152:153], 0.125, None, ALU.mult), reads=[spl], writes=[gqs])
            fw.op("dve", lambda e: e.tensor_tensor(cd.ap[:], cbf.ap[:, 3:4, :].to_broadcast([128, 32, 128]),
                                                   spl.ap[:, l, 64:96].unsqueeze(2).to_broadcast([128, 32, 128]), ALU.mult),
                  reads=[cbf, spl], writes=[cd])
            if l + 1 < n_layers:
                precast(l + 1)

        with fw.scope():
            make_pools()
            P.sq8 = fw.sbuf("sq8", [128, 8, 512], BF16)
            xt = fw.sbuf("xt", [128, 8, 512], F32)
            xn = fw.sbuf("xn", [128, 8, 512], BF16)
            xa = fw.sbuf("xa", [128, 8, 512], BF16)
            kst = fw.sbuf("kst", [128, 4, 512], BF16)
            vst = fw.sbuf("vst", [128, 4, 256], BF16)
            cos_t = fw.sbuf("cos_t", [128, 512], F32)
            sin_t = fw.sbuf("sin_t", [128, 512], F32)
            for ti, (t0, n, s) in enumerate(TILES):
                nb = n // 128
                fw.dma("sp", xt.ap[:, :, 0:n], x_src[:, t0:t0 + n].rearrange("(k p) t -> p k t", p=128),
                       reads=[T_xres[ti]], writes=[xt])
                fw.dma("sp", cos_t.ap[:, 0:n], cos_in[:, t0:t0 + n], writes=[cos_t])
                fw.dma("sp", sin_t.ap[:, 0:n], sin_in[:, t0:t0 + n], writes=[sin_t])
                norm_mod(xt, n, gs1, 0, s, xn)
                fw.dma("pool", xn_d[:, t0:t0 + n].rearrange("(k p) t -> p k t", p=128), xn.ap[:, :, 0:n],
                       reads=[xn], writes=[T_xn[ti]])
                for half in range(2):
                    wt = load_w(win_bf[l][:, half * 512:(half + 1) * 512], T_win[l])
                    for j in range(4):
                        c = half * 4 + j
                        bk = bank()
                        proj(bk, n, wt, j * 128, xn)
                        evac_copy(xa.ap[:, c, 0:n], bk.ap[:, 0:n], [bk], [xa])
                fw.dma("pool", xa_d[:, t0:t0 + n].rearrange("(k p) t -> p k t", p=128), xa.ap[:, :, 0:n],
                       reads=[xa], writes=[T_xa[ti]])
                wt = load_w(wk2_bf[l][:, :], T_wk2[l])
                hnr_group([(wt, h * 128, spl.ap[:, l, 153:154], spl, kst.ap[:, h, 0:n], kst) for h in range(4)],
                          n, xn, cos_t, sin_t)
                fw.dma("pool", kT_d[:, :, t0:t0 + n].rearrange("h p t -> p h t"), kst.ap[:, :, 0:n],
                       reads=[kst], writes=[T_kv[ti]])
                wt = load_w(win_bf[l][:, OFF_V:OFF_V + 256], T_win[l], ncols=256)
                for b in range(nb):
                    bk = bank()

                    def fv(e, b=b, bk=bk, wt=wt):
                        r = None
                        for k in range(8):
                            r = e.matmul(bk.ap[:, 0:256], xn.ap[:, k, b * 128:(b + 1) * 128], wt.ap[:, k, 0:256],
                                         start=(k == 0), stop=(k == 7))
                        return r
                    fw.op("pe", fv, reads=[xn, wt], writes=[bk])
                    evac_copy(vst.ap[:, b, :], bk.ap[:, 0:256], [bk], [vst])
                fw.dma("pool", v_d[t0:t0 + n, :].rearrange("(b p) c -> p b c", p=128), vst.ap[:, 0:nb, :],
                       reads=[vst], writes=[T_kv[ti]])

        with fw.scope():
            XW = TT + 6
            xab = fw.sbuf("xab", [128, XW], BF16)
            xc = fw.sbuf("xc", [128, TT], F32)
            xcb = fw.sbuf("xcb", [128, TT], BF16)
            a_ts = [fw.sbuf("a_t", [128, TT], F32) for _ in range(2)]
            e2_t = fw.sbuf("e2_t", [128, TT], F32)
            u_ts = [fw.sbuf("u_t", [128, TT], F32) for _ in range(2)]
            hf = fw.sbuf("hf", [128, TT], F32)
            hb = fw.sbuf("hb", [128, TT], F32)
            yab = fw.sbuf("yab", [128, TT], BF16)
            ths = [fw.sbuf("th", [128, 512], F32) for _ in range(4)]
            thi = [0]

            def gth():
                t = ths[thi[0] % 4]
                thi[0] += 1
                return t
            fw.op("pool", lambda e: e.memset(xab.ap[:], 0.0), writes=[xab])
            for c in range(8):
                fw.dma("sp", xab.ap[:, 2:2 + CTX], xa_d[c * 128:(c + 1) * 128, 0:CTX], reads=[T_xa[0]], writes=[xab])
                fw.dma("sp", xab.ap[:, 5 + CTX:5 + TT], xa_d[c * 128:(c + 1) * 128, CTX:TT], reads=T_xa[1:], writes=[xab])
                for (t0, n, s) in TILES:
                    base = t0 if s == 1 else t0 + 3
                    bk = bank()

                    def fc(e, bk=bk, base=base, n=n):
                        r = None
                        for k in range(4):
                            r = e.matmul(bk.ap[:, 0:n], cd.ap[:, k * 8 + c, :], xab.ap[:, base + k:base + k + n],
                                         start=(k == 0), stop=(k == 3))
                        return r
                    fw.op("pe", fc, reads=[cd, xab], writes=[bk])
                    fw.op("act", lambda e: e.activation(xc.ap[:, t0:t0 + n], bk.ap[:, 0:n], AF.Identity,
                                                        bias=spl.ap[:, l, 96 + c:97 + c]), reads=[bk, spl], writes=[xc])
                    fw.op("pool", lambda e: e.tensor_copy(xcb.ap[:, t0:t0 + n], xc.ap[:, t0:t0 + n]), reads=[xc], writes=[xcb])
                for d in (1, 0):
                    ci = d * 8 + c
                    a_t = a_ts[d]
                    u_t = u_ts[d]
                    for (t0, n, s) in TILES:
                        bkr = bank()
                        fw.op("pe", lambda e: e.matmul(bkr.ap[:, 0:n], gw.ap[:, (d * 2 + 0) * 8 + c, :], xcb.ap[:, t0:t0 + n],
                                                       start=True, stop=True), reads=[gw, xcb], writes=[bkr])
                        bki = bank()
                        fw.op("pe", lambda e: e.matmul(bki.ap[:, 0:n], gw.ap[:, (d * 2 + 1) * 8 + c, :], xcb.ap[:, t0:t0 + n],
                                                       start=True, stop=True), reads=[gw, xcb], writes=[bki])
                        th = gth()
                        fw.op("act", lambda e: e.activation(th.ap[:, 0:n], bkr.ap[:, 0:n], AF.Tanh, bias=hba.ap[:, ci:ci + 1],
                                                            scale=0.5), reads=[bkr, hba], writes=[th])
                        fw.op("act", lambda e: e.activation(a_t.ap[:, t0:t0 + n], th.ap[:, 0:n], AF.Exp,
                                                            bias=hclam.ap[:, ci:ci + 1], scale=hclam.ap[:, ci:ci + 1]),
                              reads=[th, hclam], writes=[a_t])
                        fw.op("act", lambda e: e.activation(e2_t.ap[:, t0:t0 + n], th.ap[:, 0:n], AF.Exp,
                                                            bias=clam.ap[:, ci:ci + 1], scale=clam.ap[:, ci:ci + 1]),
                              reads=[th, clam], writes=[e2_t])
                        th2 = gth()
                        fw.op("act", lambda e: e.activation(th2.ap[:, 0:n], bki.ap[:, 0:n], AF.Tanh, bias=hbx.ap[:, ci:ci + 1],
                                                            scale=0.5), reads=[bki, hbx], writes=[th2])
                        fw.op("dve", lambda e: e.scalar_tensor_tensor(u_t.ap[:, t0:t0 + n], th2.ap[:, 0:n], 1.0,
                                                                        xc.ap[:, t0:t0 + n], ALU.add, ALU.mult),
                              reads=[th2, xc], writes=[u_t])
                    fw.op("act", lambda e: e.activation(e2_t.ap[:], e2_t.ap[:], AF.Sqrt, bias=1.0, scale=-1.0),
                          reads=[e2_t], writes=[e2_t])
                    fw.op("dve", lambda e: e.scalar_tensor_tensor(u_t.ap[:], u_t.ap[:], 0.5, e2_t.ap[:], ALU.mult, ALU.mult),
                          reads=[u_t, e2_t], writes=[u_t])
                    if d == 0:
                        fw.op("dve", lambda e: e.tensor_tensor_scan(hf.ap[:], a_t.ap[:], u_t.ap[:], 0.0, ALU.mult, ALU.add),
                              reads=[a_t, u_t], writes=[hf])
                    else:
                        fw.op("dve", lambda e: e.tensor_tensor_scan(hb.ap[:, 0:CTX][:, ::-1], a_t.ap[:, 0:CTX][:, ::-1],
                                                                    u_t.ap[:, 0:CTX][:, ::-1], 0.0, ALU.mult, ALU.add),
                              reads=[a_t, u_t], writes=[hb])
                        fw.op("dve", lambda e: e.tensor_tensor_scan(hb.ap[:, CTX:TT][:, ::-1], a_t.ap[:, CTX:TT][:, ::-1],
                                                                    u_t.ap[:, CTX:TT][:, ::-1], hb.ap[:, 0:1], ALU.mult, ALU.add),
                              reads=[a_t, u_t, hb], writes=[hb])
                fw.op("pool", lambda e: e.tensor_tensor(yab.ap[:], hf.ap[:], hb.ap[:], ALU.add), reads=[hf, hb], writes=[yab])
                fw.dma("pool", ya_d[c * 128:(c + 1) * 128, :], yab.ap[:], reads=[yab], writes=[T_ya[c]])

        with fw.scope():
            make_pools()
            P.sq8 = fw.sbuf("sq8", [128, 8, 512], BF16)
            xt = fw.sbuf("xt", [128, 8, 512], F32)
            xn = fw.sbuf("xn", [128, 8, 512], BF16)
            ybT = fw.sbuf("ybT", [128, 8, 512], BF16)
            ycT = fw.sbuf("ycT", [128, 8, 512], BF16)
            for ti, (t0, n, s) in enumerate(TILES):
                if last and s == 1:
                    continue
                nb = n // 128
                fw.dma("sp", xn.ap[:, :, 0:n], xn_d[:, t0:t0 + n].rearrange("(k p) t -> p k t", p=128),
                       reads=[T_xn[ti]], writes=[xn])
                fw.dma("sp", xt.ap[:, :, 0:n], x_src[:, t0:t0 + n].rearrange("(k p) t -> p k t", p=128),
                       reads=[T_xres[ti]], writes=[xt])
                with fw.scope():
                    uT = fw.sbuf("uT", [128, 8, 512], BF16)
                    vg = [fw.sbuf("vg", [128, D], F32) for _ in range(4)]
                    vnb = fw.sbuf("vnb", [128, 4, D], BF16)
                    st6 = fw.sbuf("st6", [128, 4, 2, 6], F32)
                    mv = fw.sbuf("mv", [128, 4, 2], F32)
                    rsd = fw.sbuf("rsd", [128, 4], F32)
                    for half in range(2):
                        wt = load_w(win_bf[l][:, OFF_U + half * 512:OFF_U + (half + 1) * 512], T_win[l])
                        for j in range(4):
                            bk = bank()
                            proj(bk, n, wt, j * 128, xn)
                            fw.op("act", lambda e: e.activation(uT.ap[:, half * 4 + j, 0:n], bk.ap[:, 0:n], AF.Gelu_apprx_tanh),
                                  reads=[bk], writes=[uT])
                    wv = [load_w(win_bf[l][:, OFF_SV + half * 512:OFF_SV + (half + 1) * 512], T_win[l]) for half in range(2)]
                    for b in range(nb):
                        vgt = vg[b]
                        for half in range(2):
                            bk = bank()

                            def fv2(e, bk=bk, b=b, half=half):
                                r = None
                                for k in range(8):
                                    r = e.matmul(bk.ap[:, :], xn.ap[:, k, b * 128:(b + 1) * 128], wv[half].ap[:, k, :],
                                                 start=(k == 0), stop=(k == 7))
                                return r
                            fw.op("pe", fv2, reads=[xn, wv[half]], writes=[bk])
                            fw.op("act", lambda e: e.activation(vgt.ap[:, half * 512:(half + 1) * 512], bk.ap[:, :], AF.Gelu_apprx_tanh),
                                  reads=[bk], writes=[vgt])
                    for b in range(nb):
                        vgt = vg[b]
                        for half in range(2):
                            fw.op("dve", lambda e: e.bn_stats(st6.ap[:, b, half, :], vgt.ap[:, half * 512:(half + 1) * 512]),
                                  reads=[vgt], writes=[st6])
                        fw.op("dve", lambda e: e.bn_aggr(mv.ap[:, b, :], st6.ap[:, b, :, :].rearrange("p a b -> p (a b)")),
                              reads=[st6], writes=[mv])
                    for b in range(nb):
                        fw.op("act", lambda e: e.activation(rsd.ap[:, b:b + 1], mv.ap[:, b, 1:2], AF.Ln, bias=eps_t.ap[:, 0:1]),
                              reads=[mv, eps_t], writes=[rsd])
                        fw.op("act", lambda e: e.activation(rsd.ap[:, b:b + 1], rsd.ap[:, b:b + 1], AF.Exp, scale=-0.5),
                              reads=[rsd], writes=[rsd])
                    for b in range(nb):
                        vgt = vg[b]
                        fw.op("dve", lambda e: e.tensor_scalar(vgt.ap[:], vgt.ap[:], mv.ap[:, b, 0:1], rsd.ap[:, b:b + 1],
                                                               ALU.subtract, ALU.mult), reads=[vgt, mv, rsd], writes=[vgt])
                        fw.op("pool", lambda e: e.tensor_tensor(vgt.ap[:], vgt.ap[:], lng_bc.ap[:], ALU.mult),
                              reads=[vgt, lng_bc], writes=[vgt])
                        fw.op("dve", lambda e: e.tensor_tensor(vnb.ap[:, b, :], vgt.ap[:], lnb_bc.ap[:], ALU.add),
                              reads=[vgt, lnb_bc], writes=[vnb])
                    for g in range(8):
                        bk = bank()

                        def fs(e, bk=bk, g=g):
                            r = None
                            for b in range(nb):
                                r = e.matmul(bk.ap[:, b * 128:(b + 1) * 128], vnb.ap[:, b, g * 128:(g + 1) * 128], wsT.ap[:, g, :],
                                             start=True, stop=True)
                            return r
                        fw.op("pe", fs, reads=[vnb, wsT], writes=[bk])
                        tmp = g32()
                        fw.op("dve", lambda e: e.tensor_tensor(tmp.ap[:, 0:n].rearrange("p (b q) -> p b q", q=128),
                                                               bk.ap[:, 0:n].rearrange("p (b q) -> p b q", q=128),
                                                               bs_bc.ap[:, g:g + 1, :].to_broadcast([128, nb, 128]), ALU.add),
                              reads=[bk, bs_bc], writes=[tmp])
                        fw.op("pool", lambda e: e.tensor_tensor(ybT.ap[:, g, 0:n], tmp.ap[:, 0:n], uT.ap[:, g, 0:n], ALU.mult),
                              reads=[tmp, uT], writes=[ybT])
                    if dbg:
                        fw.dma("pool", dbg_yb[:, t0:t0 + n].rearrange("(k p) t -> p k t", p=128), ybT.ap[:, :, 0:n],
                               reads=[ybT], writes=[T_dbg])
                with fw.scope():
                    qT = fw.sbuf("qT", [128, 8, 2, 512], BF16)
                    fw.op("pool", lambda e: e.memset(qT.ap[:], 0.0), writes=[qT])
                    cos_t = fw.sbuf("cos_t", [128, 512], F32)
                    sin_t = fw.sbuf("sin_t", [128, 512], F32)
                    Es = [fw.sbuf("E", [128, 5, 256], BF16) for _ in range(2)]
                    kwin = fw.sbuf("kwin", [128, 4, 8 * 128], BF16)
                    vwin = fw.sbuf("vwin", [128, 8, 256], BF16)
                    dn = [fw.sbuf("dn", [128, 128], F32) for _ in range(2)]
                    fw.dma("sp", cos_t.ap[:, 0:n], cos_in[:, t0:t0 + n], writes=[cos_t])
                    fw.dma("sp", sin_t.ap[:, 0:n], sin_in[:, t0:t0 + n], writes=[sin_t])
                    fw.dma("sp", kwin.ap[:, :, 0:CTX], kT_d[:, :, 0:CTX].rearrange("h p t -> p h t"), reads=T_kv, writes=[kwin])
                    fw.dma("sp", vwin.ap[:, 0:2, :], v_d[0:CTX, :].rearrange("(b p) c -> p b c", p=128), reads=T_kv, writes=[vwin])
                    if s == 0:
                        q0 = (t0 - CTX) // 128
                        lb0 = max(q0 - 1, 0)
                        lb1 = min(q0 + nb, S // 128 - 1)
                        nlb = lb1 - lb0 + 1
                        fw.dma("sp", kwin.ap[:, :, CTX:CTX + nlb * 128],
                               kT_d[:, :, CTX + lb0 * 128:CTX + (lb1 + 1) * 128].rearrange("h p t -> p h t"),
                               reads=T_kv, writes=[kwin])
                        fw.dma("sp", vwin.ap[:, 2:2 + nlb, :],
                               v_d[CTX + lb0 * 128:CTX + (lb1 + 1) * 128, :].rearrange("(b p) c -> p b c", p=128),
                               reads=T_kv, writes=[vwin])
                    for half in range(2):
                        wt = load_w(win_bf[l][:, OFF_Q + half * 512:OFF_Q + (half + 1) * 512], T_win[l])
                        hnr_group([(wt, j * 128, gqs.ap[:, 0:1], gqs,
                                    (qT.ap[0:64, half * 4 + j, 0, 0:n], qT.ap[64:128, half * 4 + j, 1, 0:n]), qT) for j in range(4)],
                                  n, xn, cos_t, sin_t)
                    its = []
                    for qb in range(nb):
                        if s == 1:
                            keys = [(0, None), (1, None)]
                        else:
                            qg = q0 + qb
                            keys = [(0, None), (1, None)]
                            if qg > 0:
                                keys.append((2 + qg - 1 - lb0, 0))
                            keys.append((2 + qg - lb0, None))
                            if qg < S // 128 - 1:
                                keys.append((2 + qg + 1 - lb0, 1))
                        for c in range(8):
                            its.append((qb, c, keys))

                    def emit_S(i):
                        qb, c, keys = its[i]
                        h = c // 2
                        qc0 = qb * 128
                        sb = [bank() for _ in range((len(keys) + 1) // 2)]

                        def fsc(e):
                            r = None
                            for idx, (slot, mk) in enumerate(keys):
                                o = sb[idx // 2].ap[:, (idx % 2) * 256:(idx % 2) * 256 + 256]
                                if mk is not None:
                                    e.matmul(o, ident_bf, masks.ap[:, mk, :], start=True, stop=False)
                                r = e.matmul(o.rearrange("p (a b) -> p a b", a=2), kwin.ap[:, h, slot * 128:(slot + 1) * 128],
                                             qT.ap[:, c, :, qc0:qc0 + 128], start=(mk is None), stop=True)
                            return r
                        fw.op("pe", fsc, reads=[kwin, qT, masks, cbf], writes=sb)
                        return sb

                    def emit_rest(i, sb):
                        qb, c, keys = its[i]
                        h = c // 2
                        qc0 = qb * 128
                        nk = len(keys)
                        E = Es[i % 2]
                        dnt = dn[i % 2]
                        for bi in range(len(sb)):
                            wcols = 512 if (2 * bi + 1) < nk else 256
                            fw.op("act", lambda e: e.activation(
                                E.ap[:, 2 * bi:2 * bi + wcols // 256, :].rearrange("p a b -> p (a b)"),
                                sb[bi].ap[:, 0:wcols], AF.Exp), reads=[sb[bi]], writes=[E])
                        ob = bank()

                        def fo(e):
                            r = None
                            for idx, (slot, mk) in enumerate(keys):
                                e.matmul(ob.ap[0:64, 0:128], vwin.ap[:, slot, h * 64:(h + 1) * 64], E.ap[:, idx, 0:128],
                                         start=(idx == 0), stop=(idx == nk - 1))
                                r = e.matmul(ob.ap[64:128, 0:128], vwin.ap[:, slot, h * 64:(h + 1) * 64], E.ap[:, idx, 128:256],
                                             start=(idx == 0), stop=(idx == nk - 1))
                            for idx, (slot, mk) in enumerate(keys):
                                e.matmul(ob.ap[0:64, 128:256], ones_bf[:, 0:64], E.ap[:, idx, 0:128],
                                         start=(idx == 0), stop=(idx == nk - 1))
                                r = e.matmul(ob.ap[64:128, 128:256], ones_bf[:, 0:64], E.ap[:, idx, 128:256],
                                             start=(idx == 0), stop=(idx == nk - 1))
                            return r
                        fw.op("pe", fo, reads=[vwin, E, cbf], writes=[ob])
                        fw.op("dve", lambda e: e.tensor_scalar(dnt.ap[:], ob.ap[:, 128:256], esink.ap[:, c:c + 1], None, ALU.add),
                              reads=[ob, esink], writes=[dnt])
                        fw.op("dve", lambda e: e.reciprocal(dnt.ap[:], dnt.ap[:]), reads=[dnt], writes=[dnt])
                        fw.op("dve", lambda e: e.tensor_tensor(ycT.ap[:, c, qc0:qc0 + 128], ob.ap[:, 0:128], dnt.ap[:], ALU.mult),
                              reads=[ob, dnt], writes=[ycT])

                    nxt = emit_S(0)
                    for i in range(len(its)):
                        cur = nxt
                        if i + 1 < len(its):
                            nxt = emit_S(i + 1)
                        emit_rest(i, cur)
                    if dbg:
                        fw.dma("pool", dbg_yc[:, t0:t0 + n].rearrange("(k p) t -> p k t", p=128), ycT.ap[:, :, 0:n],
                               reads=[ycT], writes=[T_dbg])
                with fw.scope():
                    yaT = fw.sbuf("yaT", [128, 8, 512], BF16)
                    acc = fw.sbuf("acc", [128, 8, 512], F32)
                    mT = fw.sbuf("mT", [128, 8, 512], BF16)
                    fw.dma("sp", yaT.ap[:, :, 0:n], ya_d[:, t0:t0 + n].rearrange("(k p) t -> p k t", p=128),
                           reads=T_ya, writes=[yaT])
                    for br, ysrc in enumerate((yaT, ybT, ycT)):
                        for half in range(2):
                            wb = load_w(wbr_bf[l][br, :, half * 512:(half + 1) * 512], T_wbr[l])
                            wg = load_w(win_bf[l][:, OFF_G + br * 1024 + half * 512:OFF_G + br * 1024 + (half + 1) * 512], T_win[l])
                            for j in range(4):
                                jc = half * 4 + j
                                bp = bank()
                                proj(bp, n, wb, j * 128, ysrc)
                                bg = bank()
                                proj(bg, n, wg, j * 128, xn)
                                sg = g32()
                                fw.op("act", lambda e: e.activation(sg.ap[:, 0:n], bg.ap[:, 0:n], AF.Sigmoid), reads=[bg], writes=[sg])
                                if br == 0:
                                    fw.op("dve", lambda e: e.tensor_tensor(acc.ap[:, jc, 0:n], bp.ap[:, 0:n], sg.ap[:, 0:n], ALU.mult),
                                          reads=[bp, sg], writes=[acc])
                                else:
                                    tm = g32()
                                    fw.op("dve", lambda e: e.tensor_tensor(tm.ap[:, 0:n], bp.ap[:, 0:n], sg.ap[:, 0:n], ALU.mult),
                                          reads=[bp, sg], writes=[tm])
                                    if br == 1:
                                        fw.op("pool", lambda e: e.tensor_tensor(acc.ap[:, jc, 0:n], acc.ap[:, jc, 0:n], tm.ap[:, 0:n], ALU.add),
                                              reads=[acc, tm], writes=[acc])
                                    else:
                                        fw.op("pool", lambda e: e.tensor_tensor(mT.ap[:, jc, 0:n], acc.ap[:, jc, 0:n], tm.ap[:, 0:n], ALU.add),
                                              reads=[acc, tm], writes=[mT])
                    for half in range(2):
                        wo = load_w(wout_bf[l][:, half * 512:(half + 1) * 512], T_wout[l])
                        for j in range(4):
                            jc = half * 4 + j
                            bk = bank()
                            proj(bk, n, wo, j * 128, mT)
                            fw.op("dve", lambda e: e.scalar_tensor_tensor(xt.ap[:, jc, 0:n], bk.ap[:, 0:n], modT.ap[:, 16 + jc, s:s + 1],
                                                                            xt.ap[:, jc, 0:n], ALU.mult, ALU.add),
                                  reads=[bk, modT, xt], writes=[xt])
                    if dbg:
                        fw.dma("pool", dbg_x1[:, t0:t0 + n].rearrange("(k p) t -> p k t", p=128), xt.ap[:, :, 0:n],
                               reads=[xt], writes=[T_dbg])
                with fw.scope():
                    fT = fw.sbuf("fT", [128, 32, 512], BF16)
                    norm_mod(xt, n, gs2, 24, s, xn)
                    for q4 in range(8):
                        w1 = load_w(wff1_bf[l][:, q4 * 512:(q4 + 1) * 512], T_wff1[l])
                        for j in range(4):
                            f = q4 * 4 + j
                            bk = bank()
                            proj(bk, n, w1, j * 128, xn)
                            r = g32()
                            fw.op("act", lambda e: e.activation(r.ap[:, 0:n], bk.ap[:, 0:n], AF.Relu), reads=[bk], writes=[r])
                            fw.op("pool", lambda e: e.tensor_tensor(fT.ap[:, f, 0:n], r.ap[:, 0:n], r.ap[:, 0:n], ALU.mult),
                                  reads=[r], writes=[fT])
                    for half in range(2):
                        accb = [bank() for _ in range(4)]
                        for kg in range(4):
                            w2 = load_w(wff2_bf[l][kg * 1024:(kg + 1) * 1024, half * 512:(half + 1) * 512], T_wff2[l])

                            def f2(e, w2=w2, kg=kg, accb=accb):
                                r = None
                                for k in range(8):
                                    for j in range(4):
                                        r = e.matmul(accb[j].ap[:, 0:n], w2.ap[:, k, j * 128:(j + 1) * 128], fT.ap[:, kg * 8 + k, 0:n],
                                                     start=(kg == 0 and k == 0), stop=(kg == 3 and k == 7))
                                return r
                            fw.op("pe", f2, reads=[w2, fT], writes=accb)
                        for j in range(4):
                            jc = half * 4 + j
                            fw.op("dve", lambda e: e.scalar_tensor_tensor(xt.ap[:, jc, 0:n], accb[j].ap[:, 0:n], modT.ap[:, 40 + jc, s:s + 1],
                                                                            xt.ap[:, jc, 0:n], ALU.mult, ALU.add),
                                  reads=[accb[j], modT, xt], writes=[xt])
                    if last:
                        fw.dma("pool", outT[:, t0 - CTX:t0 - CTX + n].rearrange("(k p) t -> p k t", p=128), xt.ap[:, :, 0:n],
                               reads=[xt], writes=[T_out])
                    else:
                        fw.dma("pool", xres[:, t0:t0 + n].rearrange("(k p) t -> p k t", p=128), xt.ap[:, :, 0:n],
                               reads=[xt], writes=[T_xres[ti]])

    fw.barrier()
    fw.close()
    return nc


def _consts():
    bf = ml_dtypes.bfloat16
    cb = np.zeros((128, 4, 128), np.float32)
    cb[:, 0, :] = 1.0
    for p in range(128):
        for m in range(128):
            if p // 64 == m // 64:
                cb[p, 1, m] = 1.0
    for m in range(128):
        d = m % 64
        half = (d % 32) // 16
        if half == 0:
            cb[m + 16, 2, m] = -1.0
        else:
            cb[m - 16, 2, m] = 1.0
    cb[:, 3, :] = np.eye(128, dtype=np.float32)
    j = np.arange(128)[:, None]
    i = np.arange(128)[None, :]
    mk = np.zeros((128, 2, 256), np.float32)
    mk[:, 0, :] = np.tile(np.where(j >= i, 0.0, -30000.0).astype(np.float32), (1, 2))
    mk[:, 1, :] = np.tile(np.where(j <= i, 0.0, -30000.0).astype(np.float32), (1, 2))
    pos = np.arange(S)
    row = (pos // GRID_W).astype(np.float32)
    col = (pos % GRID_W).astype(np.float32)
    inv = np.power(np.float32(10000.0), -np.arange(16, dtype=np.float32) / np.float32(16)).astype(np.float32)
    cosT = np.ones((128, TT), np.float32)
    sinT = np.zeros((128, TT), np.float32)
    for p in range(128):
        d = p % 64
        axis = d // 32
        f = d % 16
        ang = (row if axis == 0 else col) * inv[f]
        cosT[p, CTX:] = np.cos(ang)
        sinT[p, CTX:] = np.sin(ang)
    return cb.astype(bf), mk.astype(bf), cosT, sinT


def _small_params(inp):
    sp = np.zeros((128, NL, NSP), np.float32)
    for l in range(NL):
        def cm(v):
            return np.asarray(v, np.float32).reshape(-1, 128).T
        sp[:, l, 0:48] = cm(inp["b_mod"][l])
        sp[:, l, 48:56] = cm(inp["g_norm1"][l])
        sp[:, l, 56:64] = cm(inp["g_norm2"][l])
        for k in range(4):
            sp[:, l, 64 + k * 8:72 + k * 8] = cm(inp["conv_w"][l, k])
        sp[:, l, 96:104] = cm(inp["conv_b"][l])
        for d in range(2):
            sp[:, l, 104 + d * 8:112 + d * 8] = cm(inp["lru_ba"][l, d])
            sp[:, l, 120 + d * 8:128 + d * 8] = cm(inp["lru_bx"][l, d])
            sp[:, l, 136 + d * 8:144 + d * 8] = cm(inp["lru_lambda"][l, d])
        sp[:, l, 152] = np.tile(np.asarray(inp["q_norm_g"][l], np.float32), 2)
        sp[:, l, 153] = np.tile(np.asarray(inp["k_norm_g"][l], np.float32), 2)
        sk = np.asarray(inp["sink"][l], np.float32)
        for c in range(8):
            sp[0:64, l, 154 + c] = sk[2 * c]
            sp[64:128, l, 154 + c] = sk[2 * c + 1]
    return sp


def make_in_maps(inp, n_cores=8):
    inp = {k: np.asarray(v) for k, v in inp.items()}
    cb, mk, cosT, sinT = _consts()
    sp = _small_params(inp)
    shared = {
        "sp_all": sp,
        "w_mod": np.ascontiguousarray(inp["w_mod"], np.float32),
        "w_in": np.ascontiguousarray(inp["w_in"], np.float32),
        "lru_wa": np.ascontiguousarray(inp["lru_wa"], np.float32),
        "lru_wx": np.ascontiguousarray(inp["lru_wx"], np.float32),
        "sgu_ln_g": np.ascontiguousarray(inp["sgu_ln_g"], np.float32),
        "sgu_ln_b": np.ascontiguousarray(inp["sgu_ln_b"], np.float32),
        "sgu_b": np.ascontiguousarray(inp["sgu_b"], np.float32).reshape(NL, D),
        "sgu_wT": np.ascontiguousarray(np.transpose(inp["sgu_w"], (0, 1, 3, 2)), np.float32),
        "w_branch": np.ascontiguousarray(inp["w_branch"], np.float32),
        "w_out": np.ascontiguousarray(inp["w_out"], np.float32),
        "w_ff1": np.ascontiguousarray(inp["w_ff1"], np.float32),
        "w_ff2": np.ascontiguousarray(inp["w_ff2"], np.float32),
        "cbf": cb, "masks": mk, "cosT": cosT, "sinT": sinT,
    }
    maps = []
    for core in range(n_cores):
        b = core % 4
        xT = np.ascontiguousarray(np.concatenate([inp["ctx"][b], inp["x"][b]], axis=0).T, np.float32)
        cond = np.stack([inp["c"][b], inp["c_ctx"]], axis=-1).astype(np.float32)
        cond = np.ascontiguousarray(cond.reshape(8, 128, 2).transpose(1, 0, 2))
        m = dict(shared)
        m["xT"] = xT
        m["cond"] = cond
        maps.append(m)
    return maps


_NC_CACHE = {}


def kernel(**inputs):
    maps = make_in_maps(inputs)
    if "nc" not in _NC_CACHE:
        _NC_CACHE["nc"] = build()
    nc = _NC_CACHE["nc"]
    res = run_bass_kernel_spmd(nc, maps, core_ids=list(range(8)))
    out = np.stack([np.ascontiguousarray(res.results[b]["outT"].T) for b in range(4)], axis=0)
    return out.astype(np.float32)
```

```python
import contextlib
import numpy as np
import ml_dtypes
import concourse.bass as bass
import concourse.mybir as mybir
from concourse.bass_utils import run_bass_kernel_spmd

F32 = mybir.dt.float32
BF16 = mybir.dt.bfloat16
AF = mybir.ActivationFunctionType
ALU = mybir.AluOpType

D = 1024
S = 4096
CTX = 256
TT = S + CTX
NL = 4
INW = 7680
DFF = 4096
OFF_U = 1024
OFF_SV = 2048
OFF_Q = 3072
OFF_K = 4096
OFF_V = 4352
OFF_G = 4608
NSP = 162
EPS = 1e-6
GRID_W = 64
TILES = [(0, 256, 1)] + [(256 + 512 * i, 512, 0) for i in range(8)]


class T:
    __slots__ = ("ap", "w", "r", "name")

    def __init__(self, ap=None, name=""):
        self.ap = ap
        self.w = {}
        self.r = {}
        self.name = name


class FW:
    def __init__(self, nc, n_dma_sems=12):
        self.nc = nc
        self.stack = [contextlib.ExitStack()]
        self.engs = {"pe": nc.tensor, "act": nc.scalar, "dve": nc.vector,
                     "pool": nc.gpsimd, "sp": nc.sync}
        self.sems = {}
        self.cnt = {}
        self.waited = {}
        for e in self.engs:
            self.sems[e] = self.stack[0].enter_context(nc.semaphore("s_" + e))
            self.cnt[e] = 0
        self.dma_pool = {}
        self.dma_rr = {}
        for q in ("sp", "pool"):
            ks = []
            for i in range(n_dma_sems):
                k = "d_%s_%d" % (q, i)
                self.sems[k] = self.stack[0].enter_context(nc.semaphore(k))
                self.cnt[k] = 0
                ks.append(k)
            self.dma_pool[q] = ks
            self.dma_rr[q] = 0
        self.uid = 0

    def sbuf(self, name, shape, dtype):
        self.uid += 1
        t = self.stack[-1].enter_context(self.nc.sbuf_tensor("%s_%d" % (name, self.uid), list(shape), dtype))
        return T(t, name)

    def psum(self, name, shape, dtype=F32):
        t = self.stack[-1].enter_context(self.nc.psum_tensor(name, list(shape), dtype))
        return T(t, name)

    def _wait(self, e, k, v):
        if self.waited.get((e, k), 0) >= v:
            return
        self.engs[e].wait_ge(self.sems[k], v)
        self.waited[(e, k)] = v

    def _deps(self, e, reads, writes):
        for t in reads:
            for k, v in t.w.items():
                self._wait(e, k, v)
        for t in writes:
            for k, v in t.w.items():
                self._wait(e, k, v)
            for k, v in t.r.items():
                self._wait(e, k, v)

    def _mark(self, k, v, reads, writes):
        for t in reads:
            if t.r.get(k, 0) < v:
                t.r[k] = v
        for t in writes:
            if t.w.get(k, 0) < v:
                t.w[k] = v

    def op(self, e, fn, reads=(), writes=()):
        self._deps(e, reads, writes)
        inst = fn(self.engs[e])
        self.cnt[e] += 1
        inst.then_inc(self.sems[e], 1)
        self._mark(e, self.cnt[e], reads, writes)
        return inst

    def dma(self, q, out, in_, reads=(), writes=(), **kw):
        self._deps(q, reads, writes)
        ks = self.dma_pool[q]
        k = ks[self.dma_rr[q] % len(ks)]
        self.dma_rr[q] += 1
        if self.cnt[k] > 0:
            self._wait(q, k, self.cnt[k])
        inst = self.engs[q].dma_start(out=out, in_=in_, **kw)
        self.cnt[k] += 16
        inst.then_inc(self.sems[k], 16)
        self._mark(k, self.cnt[k], reads, writes)
        return inst

    def barrier(self, engines=None):
        for e in (engines or self.engs):
            for k, v in self.cnt.items():
                if v > 0:
                    self._wait(e, k, v)

    @contextlib.contextmanager
    def scope(self):
        self.stack.append(contextlib.ExitStack())
        try:
            yield
        finally:
            self.barrier()
            self.stack.pop().close()

    def close(self):
        while self.stack:
            self.stack.pop().close()


def build(n_layers=NL, dbg=False):
    nc = bass.Bass("TRN2", target_bir_lowering=False)

    def din(name, shape, dt=F32):
        return nc.dram_tensor(name, list(shape), dt, kind="ExternalInput").ap()

    def dscr(name, shape, dt, out=False):
        kind = "ExternalOutput" if (out and dbg) else "Internal"
        return nc.dram_tensor(name, list(shape), dt, kind=kind).ap()

    xT_in = din("xT", [D, TT])
    cond_in = din("cond", [128, 8, 2])
    sp_in = din("sp_all", [128, NL, NSP])
    w_mod = din("w_mod", [NL, D, 6 * D])
    w_in = din("w_in", [NL, D, INW])
    lru_wa = din("lru_wa", [NL, 2, 16, 64, 64])
    lru_wx = din("lru_wx", [NL, 2, 16, 64, 64])
    sgu_g = din("sgu_ln_g", [NL, D])
    sgu_bb = din("sgu_ln_b", [NL, D])
    sgu_bs = din("sgu_b", [NL, D])
    sgu_wT = din("sgu_wT", [NL, 8, 128, 128])
    w_branch = din("w_branch", [NL, 3, D, D])
    w_out = din("w_out", [NL, D, D])
    w_ff1 = din("w_ff1", [NL, D, DFF])
    w_ff2 = din("w_ff2", [NL, DFF, D])
    cbf_in = din("cbf", [128, 4, 128], BF16)
    masks_in = din("masks", [128, 2, 256], BF16)
    cos_in = din("cosT", [128, TT])
    sin_in = din("sinT", [128, TT])
    outT = nc.dram_tensor("outT", [D, S], F32, kind="ExternalOutput").ap()

    xres = dscr("xres", [D, TT], F32, out=True)
    xn_d = dscr("xn_d", [D, TT], BF16, out=True)
    xa_d = dscr("xa_d", [D, TT], BF16, out=True)
    ya_d = dscr("ya_d", [D, TT], BF16, out=True)
    kT_d = dscr("kT_d", [4, 128, TT], BF16, out=True)
    v_d = dscr("v_d", [TT, 256], BF16, out=True)
    dbg_yb = dscr("dbg_yb", [D, TT], BF16, out=True) if dbg else None
    dbg_yc = dscr("dbg_yc", [D, TT], BF16, out=True) if dbg else None
    dbg_x1 = dscr("dbg_x1", [D, TT], F32, out=True) if dbg else None
    win_bf = [dscr("win_bf%d" % l, [D, INW], BF16) for l in range(n_layers)]
    wk2_bf = [dscr("wk2_bf%d" % l, [D, 512], BF16) for l in range(n_layers)]
    wbr_bf = [dscr("wbr_bf%d" % l, [3, D, D], BF16) for l in range(n_layers)]
    wout_bf = [dscr("wout_bf%d" % l, [D, D], BF16) for l in range(n_layers)]
    wff1_bf = [dscr("wff1_bf%d" % l, [D, DFF], BF16) for l in range(n_layers)]
    wff2_bf = [dscr("wff2_bf%d" % l, [DFF, D], BF16) for l in range(n_layers)]

    fw = FW(nc)
    NT = len(TILES)
    T_win = [T(name="win") for _ in range(n_layers)]
    T_wk2 = [T(name="wk2") for _ in range(n_layers)]
    T_wbr = [T(name="wbr") for _ in range(n_layers)]
    T_wout = [T(name="wout") for _ in range(n_layers)]
    T_wff1 = [T(name="wff1") for _ in range(n_layers)]
    T_wff2 = [T(name="wff2") for _ in range(n_layers)]
    T_xres = [T(name="xres%d" % i) for i in range(NT)]
    T_xn = [T(name="xn%d" % i) for i in range(NT)]
    T_xa = [T(name="xa%d" % i) for i in range(NT)]
    T_ya = [T(name="ya%d" % i) for i in range(8)]
    T_kv = [T(name="kv%d" % i) for i in range(NT)]
    T_out = T(name="out")
    T_dbg = T(name="dbg")

    cbf = fw.sbuf("cbf", [128, 4, 128], BF16)
    ones_bf = cbf.ap[:, 0, :]
    bones_bf = cbf.ap[:, 1, :]
    rmat_bf = cbf.ap[:, 2, :]
    ident_bf = cbf.ap[:, 3, :]
    masks = fw.sbuf("masks", [128, 2, 256], BF16)
    spl = fw.sbuf("spl", [128, NL, NSP], F32)
    condf = fw.sbuf("condf", [128, 8, 2], F32)
    condb = fw.sbuf("condb", [128, 8, 2], BF16)
    gw = fw.sbuf("gw", [128, 32, 128], BF16)
    cd = fw.sbuf("cd", [128, 32, 128], BF16)
    modT = fw.sbuf("modT", [128, 48, 2], F32)
    gs1 = fw.sbuf("gs1", [128, 8, 2], F32)
    gs2 = fw.sbuf("gs2", [128, 8, 2], F32)
    lam_t = fw.sbuf("lam_t", [128, 16], F32)
    clam = fw.sbuf("clam", [128, 16], F32)
    hclam = fw.sbuf("hclam", [128, 16], F32)
    hba = fw.sbuf("hba", [128, 16], F32)
    hbx = fw.sbuf("hbx", [128, 16], F32)
    esink = fw.sbuf("esink", [128, 8], F32)
    gqs = fw.sbuf("gqs", [128, 1], F32)
    wsT = fw.sbuf("wsT", [128, 8, 128], BF16)
    lng_bc = fw.sbuf("lng_bc", [128, D], F32)
    lnb_bc = fw.sbuf("lnb_bc", [128, D], F32)
    bs_bc = fw.sbuf("bs_bc", [128, 8, 128], F32)
    banks = [fw.psum("bank%d" % i, [128, 512]) for i in range(8)]
    bank_rr = [0]

    def bank():
        b = banks[bank_rr[0] % 8]
        bank_rr[0] += 1
        return b

    fw.dma("sp", cbf.ap[:], cbf_in, writes=[cbf])
    fw.dma("sp", masks.ap[:], masks_in, writes=[masks])
    fw.dma("sp", spl.ap[:], sp_in, writes=[spl])
    fw.dma("sp", condf.ap[:], cond_in, writes=[condf])
    fw.op("act", lambda e: e.activation(condb.ap[:], condf.ap[:], AF.Silu), reads=[condf], writes=[condb])
    fw.op("pool", lambda e: e.memset(gw.ap[:], 0.0), writes=[gw])

    ev_rr = [0]

    def evac_copy(out_ap, in_ap, reads, writes):
        ev_rr[0] += 1
        if ev_rr[0] % 2:
            fw.op("act", lambda e: e.activation(out_ap, in_ap, AF.Copy), reads=reads, writes=writes)
        else:
            fw.op("dve", lambda e: e.tensor_copy(out_ap, in_ap), reads=reads, writes=writes)

    def precast(l):
        for r in range(8):
            fw.dma("pool", win_bf[l][r * 128:(r + 1) * 128, :], w_in[l, r * 128:(r + 1) * 128, :], writes=[T_win[l]])
        for h in range(4):
            for half in range(2):
                fw.dma("pool", wk2_bf[l][:, h * 128 + half * 64:h * 128 + half * 64 + 64],
                       w_in[l, :, OFF_K + 64 * h:OFF_K + 64 * h + 64], writes=[T_wk2[l]])
        for br in range(3):
            for r in range(2):
                fw.dma("pool", wbr_bf[l][br, r * 512:(r + 1) * 512, :], w_branch[l, br, r * 512:(r + 1) * 512, :],
                       writes=[T_wbr[l]])
        for r in range(2):
            fw.dma("pool", wout_bf[l][r * 512:(r + 1) * 512, :], w_out[l, r * 512:(r + 1) * 512, :], writes=[T_wout[l]])
        for r in range(8):
            fw.dma("pool", wff1_bf[l][r * 128:(r + 1) * 128, :], w_ff1[l, r * 128:(r + 1) * 128, :], writes=[T_wff1[l]])
        for r in range(8):
            fw.dma("pool", wff2_bf[l][r * 512:(r + 1) * 512, :], w_ff2[l, r * 512:(r + 1) * 512, :], writes=[T_wff2[l]])

    precast(0)

    class Pools:
        pass

    P = Pools()

    def make_pools(nw=4, n32=12, n16=6):
        P.w = [fw.sbuf("wp", [128, 8, 512], BF16) for _ in range(nw)]
        P.wi = 0
        P.s32 = [fw.sbuf("s32", [128, 512], F32) for _ in range(n32)]
        P.i32 = 0
        P.s16 = [fw.sbuf("s16", [128, 512], BF16) for _ in range(n16)]
        P.i16 = 0

    def getw():
        t = P.w[P.wi % len(P.w)]
        P.wi += 1
        return t

    def g32():
        t = P.s32[P.i32 % len(P.s32)]
        P.i32 += 1
        return t

    def g16():
        t = P.s16[P.i16 % len(P.s16)]
        P.i16 += 1
        return t

    def load_w(src2d, srcT, ncols=512):
        wt = getw()
        fw.dma("sp", wt.ap[:, :, 0:ncols], src2d.rearrange("(k p) n -> p k n", p=128), reads=[srcT], writes=[wt])
        return wt

    def proj(bk, n, wt, col0, act, nk=8, m=128):
        def f(e):
            r = None
            for k in range(nk):
                r = e.matmul(bk.ap[0:m, 0:n], wt.ap[:, k, col0:col0 + m], act.ap[:, k, 0:n],
                             start=(k == 0), stop=(k == nk - 1))
            return r
        fw.op("pe", f, reads=[wt, act], writes=[bk])

    def rstd_from_bank(bk, n, scale):
        ln = g32()
        fw.op("act", lambda e: e.activation(ln.ap[:, 0:n], bk.ap[:, 0:n], AF.Ln, bias=eps_t.ap[:, 0:1], scale=scale),
              reads=[bk, eps_t], writes=[ln])
        rs = g32()
        fw.op("act", lambda e: e.activation(rs.ap[:, 0:n], ln.ap[:, 0:n], AF.Exp, scale=-0.5), reads=[ln], writes=[rs])
        return rs

    eps_t = fw.sbuf("eps_t", [128, 1], F32)
    fw.op("dve", lambda e: e.memset(eps_t.ap[:], EPS), writes=[eps_t])

    def norm_mod(xt, n, gs_t, sh_col0, s, out_t):
        sq = g16s8()
        fw.op("act", lambda e: e.activation(sq.ap[:, :, 0:n], xt.ap[:, :, 0:n], AF.Square), reads=[xt], writes=[sq])
        bk = bank()

        def f(e):
            r = None
            for c in range(8):
                r = e.matmul(bk.ap[:, 0:n], ones_bf, sq.ap[:, c, 0:n], start=(c == 0), stop=(c == 7))
            return r
        fw.op("pe", f, reads=[sq, cbf], writes=[bk])
        rs = rstd_from_bank(bk, n, 1.0 / D)
        for c in range(8):
            tmp = g32()
            fw.op("dve", lambda e: e.scalar_tensor_tensor(tmp.ap[:, 0:n], xt.ap[:, c, 0:n], gs_t.ap[:, c, s:s + 1],
                                                            rs.ap[:, 0:n], ALU.mult, ALU.mult),
                  reads=[xt, gs_t, rs], writes=[tmp])
            fw.op("act", lambda e: e.activation(out_t.ap[:, c, 0:n], tmp.ap[:, 0:n], AF.Identity,
                                                bias=modT.ap[:, sh_col0 + c, s:s + 1]),
                  reads=[tmp, modT], writes=[out_t])

    def g16s8():
        return P.sq8

    def hnr_group(items, n, act, cos_t, sin_t):
        st = []
        for (wt, col0, g_ap, g_T, out_ap, out_T) in items:
            bk = bank()
            proj(bk, n, wt, col0, act)
            raw = g32()
            fw.op("act", lambda e: e.activation(raw.ap[:, 0:n], bk.ap[:, 0:n], AF.Copy), reads=[bk], writes=[raw])
            sq = g16()
            fw.op("act", lambda e: e.activation(sq.ap[:, 0:n], bk.ap[:, 0:n], AF.Square), reads=[bk], writes=[sq])
            b2 = bank()
            fw.op("pe", lambda e: e.matmul(b2.ap[:, 0:n], bones_bf, sq.ap[:, 0:n], start=True, stop=True),
                  reads=[sq, cbf], writes=[b2])
            st.append({"raw": raw, "b2": b2})
        for d_, (wt, col0, g_ap, g_T, out_ap, out_T) in zip(st, items):
            rq = rstd_from_bank(d_["b2"], n, 1.0 / 64)
            raw = d_["raw"]
            qn = g32()
            fw.op("dve", lambda e: e.scalar_tensor_tensor(qn.ap[:, 0:n], raw.ap[:, 0:n], g_ap, rq.ap[:, 0:n],
                                                            ALU.mult, ALU.mult), reads=[raw, rq, g_T], writes=[qn])
            qnb = g16()
            fw.op("pool", lambda e: e.tensor_copy(qnb.ap[:, 0:n], qn.ap[:, 0:n]), reads=[qn], writes=[qnb])
            b3 = bank()
            fw.op("pe", lambda e: e.matmul(b3.ap[:, 0:n], rmat_bf, qnb.ap[:, 0:n], start=True, stop=True),
                  reads=[qnb, cbf], writes=[b3])
            d_["qn"] = qn
            d_["b3"] = b3
        for d_, (wt, col0, g_ap, g_T, out_ap, out_T) in zip(st, items):
            qn = d_["qn"]
            b3 = d_["b3"]
            t1 = g32()
            fw.op("pool", lambda e: e.tensor_tensor(t1.ap[:, 0:n], qn.ap[:, 0:n], cos_t.ap[:, 0:n], ALU.mult),
                  reads=[qn, cos_t], writes=[t1])
            t2 = g32()
            fw.op("dve", lambda e: e.tensor_tensor(t2.ap[:, 0:n], b3.ap[:, 0:n], sin_t.ap[:, 0:n], ALU.mult),
                  reads=[b3, sin_t], writes=[t2])
            if isinstance(out_ap, tuple):
                fw.op("dve", lambda e: e.tensor_tensor(out_ap[0], t1.ap[0:64, 0:n], t2.ap[0:64, 0:n], ALU.add),
                      reads=[t1, t2], writes=[out_T])
                fw.op("dve", lambda e: e.tensor_tensor(out_ap[1], t1.ap[64:128, 0:n], t2.ap[64:128, 0:n], ALU.add),
                      reads=[t1, t2], writes=[out_T])
            else:
                fw.op("dve", lambda e: e.tensor_tensor(out_ap, t1.ap[:, 0:n], t2.ap[:, 0:n], ALU.add),
                      reads=[t1, t2], writes=[out_T])

    for l in range(n_layers):
        last = (l == n_layers - 1)
        x_src = xT_in if l == 0 else xres

        with fw.scope():
            for typ, src in ((0, lru_wa), (1, lru_wx)):
                for d in range(2):
                    for half in range(2):
                        gi = (d * 2 + typ) * 8
                        dst = gw.ap[half * 64:(half + 1) * 64, gi:gi + 8, half * 64:(half + 1) * 64]
                        srcap = src[l, d].rearrange("(c two) i j -> two i c j", two=2)[half]
                        fw.dma("pool", dst, srcap, writes=[gw])
            fw.dma("pool", wsT.ap[:], sgu_wT[l].rearrange("g q p -> q g p"), writes=[wsT])
            fw.dma("sp", lng_bc.ap[:], sgu_g[l:l + 1, :].to_broadcast([128, D]), writes=[lng_bc])
            fw.dma("sp", lnb_bc.ap[:], sgu_bb[l:l + 1, :].to_broadcast([128, D]), writes=[lnb_bc])
            fw.dma("sp", bs_bc.ap[:].rearrange("p g q -> p (g q)"), sgu_bs[l:l + 1, :].to_broadcast([128, D]),
                   writes=[bs_bc])
            wms = [fw.sbuf("wm", [128, 8, 1024], BF16) for _ in range(2)]
            bkm = bank()
            for sec in range(6):
                wm = wms[sec % 2]
                fw.dma("pool", wm.ap[:], w_mod[l, :, sec * 1024:(sec + 1) * 1024].rearrange("(k p) n -> p k n", p=128),
                       writes=[wm])

                def fm(e, wm=wm, sec=sec):
                    r = None
                    for jj in range(8):
                        j = sec * 8 + jj
                        for k in range(8):
                            r = e.matmul(bkm.ap[:, 2 * j:2 * j + 2], wm.ap[:, k, jj * 128:(jj + 1) * 128],
                                         condb.ap[:, k, :], start=(k == 0), stop=(k == 7))
                    return r
                fw.op("pe", fm, reads=[wm, condb], writes=[bkm])
            fw.op("dve", lambda e: e.tensor_tensor(modT.ap[:], bkm.ap[:, 0:96].rearrange("p (j s) -> p j s", s=2),
                                                   spl.ap[:, l, 0:48].unsqueeze(2).to_broadcast([128, 48, 2]), ALU.add),
                  reads=[bkm, spl], writes=[modT])
            fw.op("dve", lambda e: e.scalar_tensor_tensor(gs1.ap[:], modT.ap[:, 8:16, :], 1.0,
                                                            spl.ap[:, l, 48:56].unsqueeze(2).to_broadcast([128, 8, 2]),
                                                            ALU.add, ALU.mult), reads=[modT, spl], writes=[gs1])
            fw.op("dve", lambda e: e.scalar_tensor_tensor(gs2.ap[:], modT.ap[:, 32:40, :], 1.0,
                                                            spl.ap[:, l, 56:64].unsqueeze(2).to_broadcast([128, 8, 2]),
                                                            ALU.add, ALU.mult), reads=[modT, spl], writes=[gs2])
            fw.op("act", lambda e: e.activation(lam_t.ap[:], spl.ap[:, l, 136:152], AF.Sigmoid), reads=[spl], writes=[lam_t])
            fw.op("act", lambda e: e.activation(lam_t.ap[:], lam_t.ap[:], AF.Ln), reads=[lam_t], writes=[lam_t])
            fw.op("dve", lambda e: e.tensor_scalar(clam.ap[:], lam_t.ap[:], 8.0, None, ALU.mult), reads=[lam_t], writes=[clam])
            fw.op("dve", lambda e: e.tensor_scalar(hclam.ap[:], lam_t.ap[:], 4.0, None, ALU.mult), reads=[lam_t], writes=[hclam])
            fw.op("dve", lambda e: e.tensor_scalar(hba.ap[:], spl.ap[:, l, 104:120], 0.5, None, ALU.mult), reads=[spl], writes=[hba])
            fw.op("dve", lambda e: e.tensor_scalar(hbx.ap[:], spl.ap[:, l, 120:136], 0.5, None, ALU.mult), reads=[spl], writes=[hbx])
            fw.op("act", lambda e: e.activation(esink.ap[:], spl.ap[:, l, 154:162], AF.Exp), reads=[spl], writes=[esink])
            fw.op("dve", lambda e: e.tensor_scalar(gqs.ap[:], spl.ap[:, l, 152:153], 0.125, None, ALU.mult), reads=[spl], writes=[gqs])
            fw.op("dve", lambda e: e.tensor_tensor(cd.ap[:], cbf.ap[:, 3:4, :].to_broadcast([128, 32, 128]),
                                                   spl.ap[:, l, 64:96].unsqueeze(2).to_broadcast([128, 32, 128]), ALU.mult),
                  reads=[cbf, spl], writes=[cd])
            if l + 1 < n_layers:
                precast(l + 1)

        with fw.scope():
            make_pools()
            P.sq8 = fw.sbuf("sq8", [128, 8, 512], BF16)
            xt = fw.sbuf("xt", [128, 8, 512], F32)
            xn = fw.sbuf("xn", [128, 8, 512], BF16)
            xa = fw.sbuf("xa", [128, 8, 512], BF16)
            kst = fw.sbuf("kst", [128, 4, 512], BF16)
            vst = fw.sbuf("vst", [128, 4, 256], BF16)
            cos_t = fw.sbuf("cos_t", [128, 512], F32)
            sin_t = fw.sbuf("sin_t", [128, 512], F32)
            for ti, (t0, n, s) in enumerate(TILES):
                nb = n // 128
                fw.dma("sp", xt.ap[:, :, 0:n], x_src[:, t0:t0 + n].rearrange("(k p) t -> p k t", p=128),
                       reads=[T_xres[ti]], writes=[xt])
                fw.dma("sp", cos_t.ap[:, 0:n], cos_in[:, t0:t0 + n], writes=[cos_t])
                fw.dma("sp", sin_t.ap[:, 0:n], sin_in[:, t0:t0 + n], writes=[sin_t])
                norm_mod(xt, n, gs1, 0, s, xn)
                fw.dma("pool", xn_d[:, t0:t0 + n].rearrange("(k p) t -> p k t", p=128), xn.ap[:, :, 0:n],
                       reads=[xn], writes=[T_xn[ti]])
                for half in range(2):
                    wt = load_w(win_bf[l][:, half * 512:(half + 1) * 512], T_win[l])
                    for j in range(4):
                        c = half * 4 + j
                        bk = bank()
                        proj(bk, n, wt, j * 128, xn)
                        evac_copy(xa.ap[:, c, 0:n], bk.ap[:, 0:n], [bk], [xa])
                fw.dma("pool", xa_d[:, t0:t0 + n].rearrange("(k p) t -> p k t", p=128), xa.ap[:, :, 0:n],
                       reads=[xa], writes=[T_xa[ti]])
                wt = load_w(wk2_bf[l][:, :], T_wk2[l])
                hnr_group([(wt, h * 128, spl.ap[:, l, 153:154], spl, kst.ap[:, h, 0:n], kst) for h in range(4)],
                          n, xn, cos_t, sin_t)
                fw.dma("pool", kT_d[:, :, t0:t0 + n].rearrange("h p t -> p h t"), kst.ap[:, :, 0:n],
                       reads=[kst], writes=[T_kv[ti]])
                wt = load_w(win_bf[l][:, OFF_V:OFF_V + 256], T_win[l], ncols=256)
                for b in range(nb):
                    bk = bank()

                    def fv(e, b=b, bk=bk, wt=wt):
                        r = None
                        for k in range(8):
                            r = e.matmul(bk.ap[:, 0:256], xn.ap[:, k, b * 128:(b + 1) * 128], wt.ap[:, k, 0:256],
                                         start=(k == 0), stop=(k == 7))
                        return r
                    fw.op("pe", fv, reads=[xn, wt], writes=[bk])
                    evac_copy(vst.ap[:, b, :], bk.ap[:, 0:256], [bk], [vst])
                fw.dma("pool", v_d[t0:t0 + n, :].rearrange("(b p) c -> p b c", p=128), vst.ap[:, 0:nb, :],
                       reads=[vst], writes=[T_kv[ti]])

        with fw.scope():
            XW = TT + 6
            xab = fw.sbuf("xab", [128, XW], BF16)
            xc = fw.sbuf("xc", [128, TT], F32)
            xcb = fw.sbuf("xcb", [128, TT], BF16)
            a_ts = [fw.sbuf("a_t", [128, TT], F32) for _ in range(2)]
            e2_t = fw.sbuf("e2_t", [128, TT], F32)
            u_ts = [fw.sbuf("u_t", [128, TT], F32) for _ in range(2)]
            hf = fw.sbuf("hf", [128, TT], F32)
            hb = fw.sbuf("hb", [128, TT], F32)
            yab = fw.sbuf("yab", [128, TT], BF16)
            ths = [fw.sbuf("th", [128, 512], F32) for _ in range(4)]
            thi = [0]

            def gth():
                t = ths[thi[0] % 4]
                thi[0] += 1
                return t
            fw.op("pool", lambda e: e.memset(xab.ap[:], 0.0), writes=[xab])
            for c in range(8):
                fw.dma("sp", xab.ap[:, 2:2 + CTX], xa_d[c * 128:(c + 1) * 128, 0:CTX], reads=[T_xa[0]], writes=[xab])
                fw.dma("sp", xab.ap[:, 5 + CTX:5 + TT], xa_d[c * 128:(c + 1) * 128, CTX:TT], reads=T_xa[1:], writes=[xab])
                for (t0, n, s) in TILES:
                    base = t0 if s == 1 else t0 + 3
                    bk = bank()

                    def fc(e, bk=bk, base=base, n=n):
                        r = None
                        for k in range(4):
                            r = e.matmul(bk.ap[:, 0:n], cd.ap[:, k * 8 + c, :], xab.ap[:, base + k:base + k + n],
                                         start=(k == 0), stop=(k == 3))
                        return r
                    fw.op("pe", fc, reads=[cd, xab], writes=[bk])
                    fw.op("act", lambda e: e.activation(xc.ap[:, t0:t0 + n], bk.ap[:, 0:n], AF.Identity,
                                                        bias=spl.ap[:, l, 96 + c:97 + c]), reads=[bk, spl], writes=[xc])
                    fw.op("act", lambda e: e.activation(xcb.ap[:, t0:t0 + n], bk.ap[:, 0:n], AF.Identity,
                                                        bias=spl.ap[:, l, 96 + c:97 + c]), reads=[bk, spl], writes=[xcb])
                for d in (1, 0):
                    ci = d * 8 + c
                    a_t = a_ts[d]
                    u_t = u_ts[d]
                    for (t0, n, s) in TILES:
                        bkr = bank()
                        fw.op("pe", lambda e: e.matmul(bkr.ap[:, 0:n], gw.ap[:, (d * 2 + 0) * 8 + c, :], xcb.ap[:, t0:t0 + n],
                                                       start=True, stop=True), reads=[gw, xcb], writes=[bkr])
                        bki = bank()
                        fw.op("pe", lambda e: e.matmul(bki.ap[:, 0:n], gw.ap[:, (d * 2 + 1) * 8 + c, :], xcb.ap[:, t0:t0 + n],
                                                       start=True, stop=True), reads=[gw, xcb], writes=[bki])
                        th = gth()
                        fw.op("act", lambda e: e.activation(th.ap[:, 0:n], bkr.ap[:, 0:n], AF.Tanh, bias=hba.ap[:, ci:ci + 1],
                                                            scale=0.5), reads=[bkr, hba], writes=[th])
                        fw.op("act", lambda e: e.activation(a_t.ap[:, t0:t0 + n], th.ap[:, 0:n], AF.Exp,
                                                            bias=hclam.ap[:, ci:ci + 1], scale=hclam.ap[:, ci:ci + 1]),
                              reads=[th, hclam], writes=[a_t])
                        fw.op("act", lambda e: e.activation(e2_t.ap[:, t0:t0 + n], th.ap[:, 0:n], AF.Exp,
                                                            bias=clam.ap[:, ci:ci + 1], scale=clam.ap[:, ci:ci + 1]),
                              reads=[th, clam], writes=[e2_t])
                        th2 = gth()
                        fw.op("act", lambda e: e.activation(th2.ap[:, 0:n], bki.ap[:, 0:n], AF.Tanh, bias=hbx.ap[:, ci:ci + 1],
                                                            scale=0.5), reads=[bki, hbx], writes=[th2])
                        fw.op("dve", lambda e: e.scalar_tensor_tensor(u_t.ap[:, t0:t0 + n], th2.ap[:, 0:n], 1.0,
                                                                        xc.ap[:, t0:t0 + n], ALU.add, ALU.mult),
                              reads=[th2, xc], writes=[u_t])
                    fw.op("act", lambda e: e.activation(e2_t.ap[:], e2_t.ap[:], AF.Sqrt, bias=1.0, scale=-1.0),
                          reads=[e2_t], writes=[e2_t])
                    fw.op("dve", lambda e: e.scalar_tensor_tensor(u_t.ap[:], u_t.ap[:], 0.5, e2_t.ap[:], ALU.mult, ALU.mult),
                          reads=[u_t, e2_t], writes=[u_t])
                    if d == 0:
                        fw.op("dve", lambda e: e.tensor_tensor_scan(hf.ap[:], a_t.ap[:], u_t.ap[:], 0.0, ALU.mult, ALU.add),
                              reads=[a_t, u_t], writes=[hf])
                    else:
                        fw.op("dve", lambda e: e.tensor_tensor_scan(hb.ap[:, 0:CTX][:, ::-1], a_t.ap[:, 0:CTX][:, ::-1],
                                                                    u_t.ap[:, 0:CTX][:, ::-1], 0.0, ALU.mult, ALU.add),
                              reads=[a_t, u_t], writes=[hb])
                        fw.op("dve", lambda e: e.tensor_tensor_scan(hb.ap[:, CTX:TT][:, ::-1], a_t.ap[:, CTX:TT][:, ::-1],
                                                                    u_t.ap[:, CTX:TT][:, ::-1], hb.ap[:, 0:1], ALU.mult, ALU.add),
                              reads=[a_t, u_t, hb], writes=[hb])
                fw.op("pool", lambda e: e.tensor_tensor(yab.ap[:], hf.ap[:], hb.ap[:], ALU.add), reads=[hf, hb], writes=[yab])
                fw.dma("pool", ya_d[c * 128:(c + 1) * 128, :], yab.ap[:], reads=[yab], writes=[T_ya[c]])

        with fw.scope():
            make_pools()
            P.sq8 = fw.sbuf("sq8", [128, 8, 512], BF16)
            xt = fw.sbuf("xt", [128, 8, 512], F32)
            xn = fw.sbuf("xn", [128, 8, 512], BF16)
            ybT = fw.sbuf("ybT", [128, 8, 512], BF16)
            ycT = fw.sbuf("ycT", [128, 8, 512], BF16)
            for ti, (t0, n, s) in enumerate(TILES):
                if last and s == 1:
                    continue
                nb = n // 128
                fw.dma("sp", xn.ap[:, :, 0:n], xn_d[:, t0:t0 + n].rearrange("(k p) t -> p k t", p=128),
                       reads=[T_xn[ti]], writes=[xn])
                fw.dma("sp", xt.ap[:, :, 0:n], x_src[:, t0:t0 + n].rearrange("(k p) t -> p k t", p=128),
                       reads=[T_xres[ti]], writes=[xt])
                with fw.scope():
                    uT = fw.sbuf("uT", [128, 8, 512], BF16)
                    vg = [fw.sbuf("vg", [128, D], F32) for _ in range(2)]
                    vnb = fw.sbuf("vnb", [128, 4, D], BF16)
                    st6 = fw.sbuf("st6", [128, 2, 6], F32)
                    mv = fw.sbuf("mv", [128, 2], F32)
                    rsd = fw.sbuf("rsd", [128, 1], F32)
                    for half in range(2):
                        wt = load_w(win_bf[l][:, OFF_U + half * 512:OFF_U + (half + 1) * 512], T_win[l])
                        for j in range(4):
                            bk = bank()
                            proj(bk, n, wt, j * 128, xn)
                            fw.op("act", lambda e: e.activation(uT.ap[:, half * 4 + j, 0:n], bk.ap[:, 0:n], AF.Gelu_apprx_tanh),
                                  reads=[bk], writes=[uT])
                    wv = [load_w(win_bf[l][:, OFF_SV + half * 512:OFF_SV + (half + 1) * 512], T_win[l]) for half in range(2)]
                    for b in range(nb):
                        vgt = vg[b % 2]
                        for half in range(2):
                            bk = bank()

                            def fv2(e, bk=bk, b=b, half=half):
                                r = None
                                for k in range(8):
                                    r = e.matmul(bk.ap[:, :], xn.ap[:, k, b * 128:(b + 1) * 128], wv[half].ap[:, k, :],
                                                 start=(k == 0), stop=(k == 7))
                                return r
                            fw.op("pe", fv2, reads=[xn, wv[half]], writes=[bk])
                            fw.op("act", lambda e: e.activation(vgt.ap[:, half * 512:(half + 1) * 512], bk.ap[:, :], AF.Gelu_apprx_tanh),
                                  reads=[bk], writes=[vgt])
                        for half in range(2):
                            fw.op("dve", lambda e: e.bn_stats(st6.ap[:, half, :], vgt.ap[:, half * 512:(half + 1) * 512]),
                                  reads=[vgt], writes=[st6])
                        fw.op("dve", lambda e: e.bn_aggr(mv.ap[:], st6.ap[:].rearrange("p a b -> p (a b)")), reads=[st6], writes=[mv])
                        fw.op("act", lambda e: e.activation(rsd.ap[:], mv.ap[:, 1:2], AF.Ln, bias=eps_t.ap[:, 0:1]),
                              reads=[mv, eps_t], writes=[rsd])
                        fw.op("act", lambda e: e.activation(rsd.ap[:], rsd.ap[:], AF.Exp, scale=-0.5), reads=[rsd], writes=[rsd])
                        fw.op("dve", lambda e: e.tensor_scalar(vgt.ap[:], vgt.ap[:], mv.ap[:, 0:1], rsd.ap[:, 0:1],
                                                               ALU.subtract, ALU.mult), reads=[vgt, mv, rsd], writes=[vgt])
                        fw.op("pool", lambda e: e.tensor_tensor(vgt.ap[:], vgt.ap[:], lng_bc.ap[:], ALU.mult),
                              reads=[vgt, lng_bc], writes=[vgt])
                        fw.op("dve", lambda e: e.tensor_tensor(vnb.ap[:, b, :], vgt.ap[:], lnb_bc.ap[:], ALU.add),
                              reads=[vgt, lnb_bc], writes=[vnb])
                    for g in range(8):
                        bk = bank()

                        def fs(e, bk=bk, g=g):
                            r = None
                            for b in range(nb):
                                r = e.matmul(bk.ap[:, b * 128:(b + 1) * 128], vnb.ap[:, b, g * 128:(g + 1) * 128], wsT.ap[:, g, :],
                                             start=True, stop=True)
                            return r
                        fw.op("pe", fs, reads=[vnb, wsT], writes=[bk])
                        tmp = g32()
                        fw.op("dve", lambda e: e.tensor_tensor(tmp.ap[:, 0:n].rearrange("p (b q) -> p b q", q=128),
                                                               bk.ap[:, 0:n].rearrange("p (b q) -> p b q", q=128),
                                                               bs_bc.ap[:, g:g + 1, :].to_broadcast([128, nb, 128]), ALU.add),
                              reads=[bk, bs_bc], writes=[tmp])
                        fw.op("pool", lambda e: e.tensor_tensor(ybT.ap[:, g, 0:n], tmp.ap[:, 0:n], uT.ap[:, g, 0:n], ALU.mult),
                              reads=[tmp, uT], writes=[ybT])
                    if dbg:
                        fw.dma("pool", dbg_yb[:, t0:t0 + n].rearrange("(k p) t -> p k t", p=128), ybT.ap[:, :, 0:n],
                               reads=[ybT], writes=[T_dbg])
                with fw.scope():
                    qT = fw.sbuf("qT", [128, 8, 2, 512], BF16)
                    fw.op("pool", lambda e: e.memset(qT.ap[:], 0.0), writes=[qT])
                    cos_t = fw.sbuf("cos_t", [128, 512], F32)
                    sin_t = fw.sbuf("sin_t", [128, 512], F32)
                    Es = [fw.sbuf("E", [128, 5, 256], BF16) for _ in range(2)]
                    kwin = fw.sbuf("kwin", [128, 4, 8 * 128], BF16)
                    vwin = fw.sbuf("vwin", [128, 8, 256], BF16)
                    dn = [fw.sbuf("dn", [128, 128], F32) for _ in range(2)]
                    fw.dma("sp", cos_t.ap[:, 0:n], cos_in[:, t0:t0 + n], writes=[cos_t])
                    fw.dma("sp", sin_t.ap[:, 0:n], sin_in[:, t0:t0 + n], writes=[sin_t])
                    fw.dma("sp", kwin.ap[:, :, 0:CTX], kT_d[:, :, 0:CTX].rearrange("h p t -> p h t"), reads=T_kv, writes=[kwin])
                    fw.dma("sp", vwin.ap[:, 0:2, :], v_d[0:CTX, :].rearrange("(b p) c -> p b c", p=128), reads=T_kv, writes=[vwin])
                    if s == 0:
                        q0 = (t0 - CTX) // 128
                        lb0 = max(q0 - 1, 0)
                        lb1 = min(q0 + nb, S // 128 - 1)
                        nlb = lb1 - lb0 + 1
                        fw.dma("sp", kwin.ap[:, :, CTX:CTX + nlb * 128],
                               kT_d[:, :, CTX + lb0 * 128:CTX + (lb1 + 1) * 128].rearrange("h p t -> p h t"),
                               reads=T_kv, writes=[kwin])
                        fw.dma("sp", vwin.ap[:, 2:2 + nlb, :],
                               v_d[CTX + lb0 * 128:CTX + (lb1 + 1) * 128, :].rearrange("(b p) c -> p b c", p=128),
                               reads=T_kv, writes=[vwin])
                    for half in range(2):
                        wt = load_w(win_bf[l][:, OFF_Q + half * 512:OFF_Q + (half + 1) * 512], T_win[l])
                        hnr_group([(wt, j * 128, gqs.ap[:, 0:1], gqs,
                                    (qT.ap[0:64, half * 4 + j, 0, 0:n], qT.ap[64:128, half * 4 + j, 1, 0:n]), qT) for j in range(4)],
                                  n, xn, cos_t, sin_t)
                    its = []
                    for qb in range(nb):
                        if s == 1:
                            keys = [(0, None), (1, None)]
                        else:
                            qg = q0 + qb
                            keys = [(0, None), (1, None)]
                            if qg > 0:
                                keys.append((2 + qg - 1 - lb0, 0))
                            keys.append((2 + qg - lb0, None))
                            if qg < S // 128 - 1:
                                keys.append((2 + qg + 1 - lb0, 1))
                        for c in range(8):
                            its.append((qb, c, keys))

                    def emit_S(i):
                        qb, c, keys = its[i]
                        h = c // 2
                        qc0 = qb * 128
                        sb = [bank() for _ in range((len(keys) + 1) // 2)]

                        def fsc(e):
                            r = None
                            for idx, (slot, mk) in enumerate(keys):
                                o = sb[idx // 2].ap[:, (idx % 2) * 256:(idx % 2) * 256 + 256]
                                if mk is not None:
                                    e.matmul(o, ident_bf, masks.ap[:, mk, :], start=True, stop=False)
                                r = e.matmul(o.rearrange("p (a b) -> p a b", a=2), kwin.ap[:, h, slot * 128:(slot + 1) * 128],
                                             qT.ap[:, c, :, qc0:qc0 + 128], start=(mk is None), stop=True)
                            return r
                        fw.op("pe", fsc, reads=[kwin, qT, masks, cbf], writes=sb)
                        return sb

                    def emit_rest(i, sb):
                        qb, c, keys = its[i]
                        h = c // 2
                        qc0 = qb * 128
                        nk = len(keys)
                        E = Es[i % 2]
                        dnt = dn[i % 2]
                        for bi in range(len(sb)):
                            wcols = 512 if (2 * bi + 1) < nk else 256
                            fw.op("act", lambda e: e.activation(
                                E.ap[:, 2 * bi:2 * bi + wcols // 256, :].rearrange("p a b -> p (a b)"),
                                sb[bi].ap[:, 0:wcols], AF.Exp), reads=[sb[bi]], writes=[E])
                        ob = bank()

                        def fo(e):
                            r = None
                            for idx, (slot, mk) in enumerate(keys):
                                e.matmul(ob.ap[0:64, 0:128], vwin.ap[:, slot, h * 64:(h + 1) * 64], E.ap[:, idx, 0:128],
                                         start=(idx == 0), stop=(idx == nk - 1))
                                r = e.matmul(ob.ap[64:128, 0:128], vwin.ap[:, slot, h * 64:(h + 1) * 64], E.ap[:, idx, 128:256],
                                             start=(idx == 0), stop=(idx == nk - 1))
                            for idx, (slot, mk) in enumerate(keys):
                                e.matmul(ob.ap[0:64, 128:256], ones_bf[:, 0:64], E.ap[:, idx, 0:128],
                                         start=(idx == 0), stop=(idx == nk - 1))
                                r = e.matmul(ob.ap[64:128, 128:256], ones_bf[:, 0:64], E.ap[:, idx, 128:256],
                                             start=(idx == 0), stop=(idx == nk - 1))
                            return r
                        fw.op("pe", fo, reads=[vwin, E, cbf], writes=[ob])
                        fw.op("dve", lambda e: e.tensor_scalar(dnt.ap[:], ob.ap[:, 128:256], esink.ap[:, c:c + 1], None, ALU.add),
                              reads=[ob, esink], writes=[dnt])
                        fw.op("dve", lambda e: e.reciprocal(dnt.ap[:], dnt.ap[:]), reads=[dnt], writes=[dnt])
                        fw.op("dve", lambda e: e.tensor_tensor(ycT.ap[:, c, qc0:qc0 + 128], ob.ap[:, 0:128], dnt.ap[:], ALU.mult),
                              reads=[ob, dnt], writes=[ycT])

                    nxt = emit_S(0)
                    for i in range(len(its)):
                        cur = nxt
                        if i + 1 < len(its):
                            nxt = emit_S(i + 1)
                        emit_rest(i, cur)
                    if dbg:
                        fw.dma("pool", dbg_yc[:, t0:t0 + n].rearrange("(k p) t -> p k t", p=128), ycT.ap[:, :, 0:n],
                               reads=[ycT], writes=[T_dbg])
                with fw.scope():
                    yaT = fw.sbuf("yaT", [128, 8, 512], BF16)
                    acc = fw.sbuf("acc", [128, 8, 512], F32)
                    mT = fw.sbuf("mT", [128, 8, 512], BF16)
                    fw.dma("sp", yaT.ap[:, :, 0:n], ya_d[:, t0:t0 + n].rearrange("(k p) t -> p k t", p=128),
                           reads=T_ya, writes=[yaT])
                    for br, ysrc in enumerate((yaT, ybT, ycT)):
                        for half in range(2):
                            wb = load_w(wbr_bf[l][br, :, half * 512:(half + 1) * 512], T_wbr[l])
                            wg = load_w(win_bf[l][:, OFF_G + br * 1024 + half * 512:OFF_G + br * 1024 + (half + 1) * 512], T_win[l])
                            for j in range(4):
                                jc = half * 4 + j
                                bp = bank()
                                proj(bp, n, wb, j * 128, ysrc)
                                bg = bank()
                                proj(bg, n, wg, j * 128, xn)
                                sg = g32()
                                fw.op("act", lambda e: e.activation(sg.ap[:, 0:n], bg.ap[:, 0:n], AF.Sigmoid), reads=[bg], writes=[sg])
                                if br == 0:
                                    fw.op("dve", lambda e: e.tensor_tensor(acc.ap[:, jc, 0:n], bp.ap[:, 0:n], sg.ap[:, 0:n], ALU.mult),
                                          reads=[bp, sg], writes=[acc])
                                else:
                                    tm = g32()
                                    fw.op("dve", lambda e: e.tensor_tensor(tm.ap[:, 0:n], bp.ap[:, 0:n], sg.ap[:, 0:n], ALU.mult),
                                          reads=[bp, sg], writes=[tm])
                                    if br == 1:
                                        fw.op("pool", lambda e: e.tensor_tensor(acc.ap[:, jc, 0:n], acc.ap[:, jc, 0:n], tm.ap[:, 0:n], ALU.add),
                                              reads=[acc, tm], writes=[acc])
                                    else:
                                        fw.op("pool", lambda e: e.tensor_tensor(mT.ap[:, jc, 0:n], acc.ap[:, jc, 0:n], tm.ap[:, 0:n], ALU.add),
                                              reads=[acc, tm], writes=[mT])
                    for half in range(2):
                        wo = load_w(wout_bf[l][:, half * 512:(half + 1) * 512], T_wout[l])
                        for j in range(4):
                            jc = half * 4 + j
                            bk = bank()
                            proj(bk, n, wo, j * 128, mT)
                            fw.op("dve", lambda e: e.scalar_tensor_tensor(xt.ap[:, jc, 0:n], bk.ap[:, 0:n], modT.ap[:, 16 + jc, s:s + 1],
                                                                            xt.ap[:, jc, 0:n], ALU.mult, ALU.add),
                                  reads=[bk, modT, xt], writes=[xt])
                    if dbg:
                        fw.dma("pool", dbg_x1[:, t0:t0 + n].rearrange("(k p) t -> p k t", p=128), xt.ap[:, :, 0:n],
                               reads=[xt], writes=[T_dbg])
                with fw.scope():
                    fT = fw.sbuf("fT", [128, 32, 512], BF16)
                    norm_mod(xt, n, gs2, 24, s, xn)
                    for q4 in range(8):
                        w1 = load_w(wff1_bf[l][:, q4 * 512:(q4 + 1) * 512], T_wff1[l])
                        for j in range(4):
                            f = q4 * 4 + j
                            bk = bank()
                            proj(bk, n, w1, j * 128, xn)
                            r = g32()
                            fw.op("act", lambda e: e.activation(r.ap[:, 0:n], bk.ap[:, 0:n], AF.Relu), reads=[bk], writes=[r])
                            fw.op("pool", lambda e: e.tensor_tensor(fT.ap[:, f, 0:n], r.ap[:, 0:n], r.ap[:, 0:n], ALU.mult),
                                  reads=[r], writes=[fT])
                    for half in range(2):
                        accb = [bank() for _ in range(4)]
                        for kg in range(4):
                            w2 = load_w(wff2_bf[l][kg * 1024:(kg + 1) * 1024, half * 512:(half + 1) * 512], T_wff2[l])

                            def f2(e, w2=w2, kg=kg, accb=accb):
                                r = None
                                for k in range(8):
                                    for j in range(4):
                                        r = e.matmul(accb[j].ap[:, 0:n], w2.ap[:, k, j * 128:(j + 1) * 128], fT.ap[:, kg * 8 + k, 0:n],
                                                     start=(kg == 0 and k == 0), stop=(kg == 3 and k == 7))
                                return r
                            fw.op("pe", f2, reads=[w2, fT], writes=accb)
                        for j in range(4):
                            jc = half * 4 + j
                            fw.op("dve", lambda e: e.scalar_tensor_tensor(xt.ap[:, jc, 0:n], accb[j].ap[:, 0:n], modT.ap[:, 40 + jc, s:s + 1],
                                                                            xt.ap[:, jc, 0:n], ALU.mult, ALU.add),
                                  reads=[accb[j], modT, xt], writes=[xt])
                    if last:
                        fw.dma("pool", outT[:, t0 - CTX:t0 - CTX + n].rearrange("(k p) t -> p k t", p=128), xt.ap[:, :, 0:n],
                               reads=[xt], writes=[T_out])
                    else:
                        fw.dma("pool", xres[:, t0:t0 + n].rearrange("(k p) t -> p k t", p=128), xt.ap[:, :, 0:n],
                               reads=[xt], writes=[T_xres[ti]])

    fw.barrier()
    fw.close()
    return nc


def _consts():
    bf = ml_dtypes.bfloat16
    cb = np.zeros((128, 4, 128), np.float32)
    cb[:, 0, :] = 1.0
    for p in range(128):
        for m in range(128):
            if p // 64 == m // 64:
                cb[p, 1, m] = 1.0
    for m in range(128):
        d = m % 64
        half = (d % 32) // 16
        if half == 0:
            cb[m + 16, 2, m] = -1.0
        else:
            cb[m - 16, 2, m] = 1.0
    cb[:, 3, :] = np.eye(128, dtype=np.float32)
    j = np.arange(128)[:, None]
    i = np.arange(128)[None, :]
    mk = np.zeros((128, 2, 256), np.float32)
    mk[:, 0, :] = np.tile(np.where(j >= i, 0.0, -30000.0).astype(np.float32), (1, 2))
    mk[:, 1, :] = np.tile(np.where(j <= i, 0.0, -30000.0).astype(np.float32), (1, 2))
    pos = np.arange(S)
    row = (pos // GRID_W).astype(np.float32)
    col = (pos % GRID_W).astype(np.float32)
    inv = np.power(np.float32(10000.0), -np.arange(16, dtype=np.float32) / np.float32(16)).astype(np.float32)
    cosT = np.ones((128, TT), np.float32)
    sinT = np.zeros((128, TT), np.float32)
    for p in range(128):
        d = p % 64
        axis = d // 32
        f = d % 16
        ang = (row if axis == 0 else col) * inv[f]
        cosT[p, CTX:] = np.cos(ang)
        sinT[p, CTX:] = np.sin(ang)
    return cb.astype(bf), mk.astype(bf), cosT, sinT


def _small_params(inp):
    sp = np.zeros((128, NL, NSP), np.float32)
    for l in range(NL):
        def cm(v):
            return np.asarray(v, np.float32).reshape(-1, 128).T
        sp[:, l, 0:48] = cm(inp["b_mod"][l])
        sp[:, l, 48:56] = cm(inp["g_norm1"][l])
        sp[:, l, 56:64] = cm(inp["g_norm2"][l])
        for k in range(4):
            sp[:, l, 64 + k * 8:72 + k * 8] = cm(inp["conv_w"][l, k])
        sp[:, l, 96:104] = cm(inp["conv_b"][l])
        for d in range(2):
            sp[:, l, 104 + d * 8:112 + d * 8] = cm(inp["lru_ba"][l, d])
            sp[:, l, 120 + d * 8:128 + d * 8] = cm(inp["lru_bx"][l, d])
            sp[:, l, 136 + d * 8:144 + d * 8] = cm(inp["lru_lambda"][l, d])
        sp[:, l, 152] = np.tile(np.asarray(inp["q_norm_g"][l], np.float32), 2)
        sp[:, l, 153] = np.tile(np.asarray(inp["k_norm_g"][l], np.float32), 2)
        sk = np.asarray(inp["sink"][l], np.float32)
        for c in range(8):
            sp[0:64, l, 154 + c] = sk[2 * c]
            sp[64:128, l, 154 + c] = sk[2 * c + 1]
    return sp


def make_in_maps(inp, n_cores=8):
    inp = {k: np.asarray(v) for k, v in inp.items()}
    cb, mk, cosT, sinT = _consts()
    sp = _small_params(inp)
    shared = {
        "sp_all": sp,
        "w_mod": np.ascontiguousarray(inp["w_mod"], np.float32),
        "w_in": np.ascontiguousarray(inp["w_in"], np.float32),
        "lru_wa": np.ascontiguousarray(inp["lru_wa"], np.float32),
        "lru_wx": np.ascontiguousarray(inp["lru_wx"], np.float32),
        "sgu_ln_g": np.ascontiguousarray(inp["sgu_ln_g"], np.float32),
        "sgu_ln_b": np.ascontiguousarray(inp["sgu_ln_b"], np.float32),
        "sgu_b": np.ascontiguousarray(inp["sgu_b"], np.float32).reshape(NL, D),
        "sgu_wT": np.ascontiguousarray(np.transpose(inp["sgu_w"], (0, 1, 3, 2)), np.float32),
        "w_branch": np.ascontiguousarray(inp["w_branch"], np.float32),
        "w_out": np.ascontiguousarray(inp["w_out"], np.float32),
        "w_ff1": np.ascontiguousarray(inp["w_ff1"], np.float32),
        "w_ff2": np.ascontiguousarray(inp["w_ff2"], np.float32),
        "cbf": cb, "masks": mk, "cosT": cosT, "sinT": sinT,
    }
    maps = []
    for core in range(n_cores):
        b = core % 4
        xT = np.ascontiguousarray(np.concatenate([inp["ctx"][b], inp["x"][b]], axis=0).T, np.float32)
        cond = np.stack([inp["c"][b], inp["c_ctx"]], axis=-1).astype(np.float32)
        cond = np.ascontiguousarray(cond.reshape(8, 128, 2).transpose(1, 0, 2))
        m = dict(shared)
        m["xT"] = xT
        m["cond"] = cond
        maps.append(m)
    return maps


_NC_CACHE = {}


def kernel(**inputs):
    maps = make_in_maps(inputs)
    if "nc" not in _NC_CACHE:
        _NC_CACHE["nc"] = build()
    nc = _NC_CACHE["nc"]
    res = run_bass_kernel_spmd(nc, maps, core_ids=list(range(8)))
    out = np.stack([np.ascontiguousarray(res.results[b]["outT"].T) for b in range(4)], axis=0)
    return out.astype(np.float32)
```
